# Optimizing a Trainium2 kernel written in Bass

```python
import math
import jax
import jax.numpy as jnp
from jax import lax
import numpy as np


D_MODEL = 1024
BATCH = 16
SEQ = 2048
DEPTH = 2

HEAD_DIM = 64
FOX_HEADS = (D_MODEL // 2) // HEAD_DIM
DIFF_HEADS = (D_MODEL // 2) // (2 * HEAD_DIM)
SWA_HEADS = D_MODEL // HEAD_DIM
SWA_KV_HEADS = SWA_HEADS // 4
SWA_GROUP = SWA_HEADS // SWA_KV_HEADS
WINDOW = 128
Q_BLOCK = 128
NUM_BUCKETS = 32
MAX_DISTANCE = 128
REL_HEADS = max(DIFF_HEADS, SWA_HEADS)
FFN_HIDDEN = -(-8 * D_MODEL // (3 * 256)) * 256
N_EVEN = (DEPTH + 1) // 2
N_ODD = DEPTH // 2
RMS_EPS = 1e-6
NEG_INF = -1e30
FOX_W = FOX_HEADS * HEAD_DIM
DIFF_W = DIFF_HEADS * 2 * HEAD_DIM
IN_SPLITS = (FOX_W, FOX_W, FOX_W, FOX_HEADS, DIFF_W, DIFF_W, DIFF_W)
IN_SPLIT_IDX = tuple(int(i) for i in np.cumsum(IN_SPLITS)[:-1])
EVEN_IN_WIDTH = sum(IN_SPLITS)
MIX_WIDTH = FOX_W + DIFF_W
SWA_Q_W = SWA_HEADS * HEAD_DIM
SWA_KV_W = SWA_KV_HEADS * HEAD_DIM
SWA_QKV_WIDTH = SWA_Q_W + 2 * SWA_KV_W

kernel_name = 'hybrid_fox_diff_swa_block'


def rms_norm(x, gain):
    xf = x.astype(jnp.float32)
    y = xf * lax.rsqrt(jnp.mean(xf * xf, axis=-1, keepdims=True) + RMS_EPS)
    return (y * gain.astype(jnp.float32)).astype(x.dtype)


def t5_bucket(delta):
    n = jnp.maximum(delta, 0)
    max_exact = NUM_BUCKETS // 2
    nf = jnp.maximum(n, 1).astype(jnp.float32)
    large = max_exact + (jnp.log(nf / max_exact) / math.log(MAX_DISTANCE / max_exact)
                         * (NUM_BUCKETS - max_exact)).astype(jnp.int32)
    large = jnp.minimum(large, NUM_BUCKETS - 1)
    return jnp.where(n < max_exact, n, large)


def rel_bias(table, delta):
    return jnp.transpose(table[t5_bucket(delta)].astype(jnp.float32), (2, 0, 1))


def fox_attention(q, k, v, c):
    S = q.shape[1]
    scale = HEAD_DIM ** -0.5
    outs = []
    for i in range(S // Q_BLOCK):
        start, end = i * Q_BLOCK, (i + 1) * Q_BLOCK
        qpos = jnp.arange(start, end)
        kpos = jnp.arange(end)
        causal = kpos[None, :] <= qpos[:, None]
        s = jnp.einsum('bqhd,bkhd->bhqk', q[:, start:end], k[:, :end]).astype(jnp.float32) * scale
        s = s + c[:, :, start:end, None] - c[:, :, None, :end]
        p = jax.nn.softmax(jnp.where(causal, s, NEG_INF), axis=-1)
        outs.append(jnp.einsum('bhqk,bkhd->bqhd', p.astype(v.dtype), v[:, :end]))
    return jnp.concatenate(outs, axis=1)


def diff_attention(q, k, v, lam, table):
    S = q.shape[1]
    scale = HEAD_DIM ** -0.5
    outs = []
    for i in range(S // Q_BLOCK):
        start, end = i * Q_BLOCK, (i + 1) * Q_BLOCK
        qpos = jnp.arange(start, end)
        kpos = jnp.arange(end)
        delta = qpos[:, None] - kpos[None, :]
        s = jnp.einsum('bqhmd,bkhmd->bhmqk', q[:, start:end], k[:, :end]).astype(jnp.float32) * scale
        s = s + rel_bias(table, delta)[None, :, None]
        p = jax.nn.softmax(jnp.where(delta >= 0, s, NEG_INF), axis=-1)
        a = p[:, :, 0] - lam * p[:, :, 1]
        outs.append(jnp.einsum('bhqk,bkhe->bqhe', a.astype(v.dtype), v[:, :end]))
    return jnp.concatenate(outs, axis=1)


def even_mixer(h, w_in, b_forget, fox_q_norm, fox_k_norm, diff_q_norm, diff_k_norm,
               lambda_q1, lambda_k1, lambda_q2, lambda_k2, diff_subln, w_out, table, lambda_init):
    B, S, _ = h.shape
    proj = h @ w_in
    fq, fk, fv, ff, dq, dk, dv = jnp.split(proj, IN_SPLIT_IDX, axis=-1)
    fq = rms_norm(fq.reshape(B, S, FOX_HEADS, HEAD_DIM), fox_q_norm)
    fk = rms_norm(fk.reshape(B, S, FOX_HEADS, HEAD_DIM), fox_k_norm)
    fv = fv.reshape(B, S, FOX_HEADS, HEAD_DIM)
    log_f = jax.nn.log_sigmoid((ff + b_forget).astype(jnp.float32))
    c = jnp.transpose(jnp.cumsum(log_f, axis=1), (0, 2, 1))
    fox_out = fox_attention(fq, fk, fv, c).reshape(B, S, FOX_W)
    dq = rms_norm(dq.reshape(B, S, DIFF_HEADS, 2, HEAD_DIM), diff_q_norm)
    dk = rms_norm(dk.reshape(B, S, DIFF_HEADS, 2, HEAD_DIM), diff_k_norm)
    dv = dv.reshape(B, S, DIFF_HEADS, 2 * HEAD_DIM)
    f32 = jnp.float32
    lam = (jnp.exp(jnp.sum(lambda_q1.astype(f32) * lambda_k1.astype(f32)))
           - jnp.exp(jnp.sum(lambda_q2.astype(f32) * lambda_k2.astype(f32))) + lambda_init)
    d_out = diff_attention(dq, dk, dv, lam, table[:, :DIFF_HEADS])
    d_out = (rms_norm(d_out, diff_subln) * (1.0 - lambda_init)).reshape(B, S, DIFF_W)
    return jnp.concatenate([fox_out, d_out], axis=-1) @ w_out


def with_prev_block(t):
    prev = jnp.concatenate([jnp.zeros_like(t[:, :1]), t[:, :-1]], axis=1)
    return jnp.concatenate([prev, t], axis=2)


def odd_mixer(h, w_qkv, q_norm, k_norm, sinks, w_out, table):
    B, S, _ = h.shape
    nb = S // WINDOW
    proj = h @ w_qkv
    q, k, v = jnp.split(proj, (SWA_Q_W, SWA_Q_W + SWA_KV_W), axis=-1)
    q = rms_norm(q.reshape(B, S, SWA_HEADS, HEAD_DIM), q_norm)
    k = rms_norm(k.reshape(B, S, SWA_KV_HEADS, HEAD_DIM), k_norm)
    v = v.reshape(B, S, SWA_KV_HEADS, HEAD_DIM)
    qb = q.reshape(B, nb, WINDOW, SWA_KV_HEADS, SWA_GROUP, HEAD_DIM)
    kk = with_prev_block(k.reshape(B, nb, WINDOW, SWA_KV_HEADS, HEAD_DIM))
    vv = with_prev_block(v.reshape(B, nb, WINDOW, SWA_KV_HEADS, HEAD_DIM))
    s = jnp.einsum('bnqhgd,bnkhd->bnhgqk', qb, kk).astype(jnp.float32) * (HEAD_DIM ** -0.5)
    a_idx = jnp.arange(WINDOW)
    b_idx = jnp.arange(2 * WINDOW)
    delta = WINDOW + a_idx[:, None] - b_idx[None, :]
    bias = rel_bias(table, delta).reshape(SWA_KV_HEADS, SWA_GROUP, WINDOW, 2 * WINDOW)
    kpos = jnp.arange(nb)[:, None, None] * WINDOW - WINDOW + b_idx[None, None, :]
    valid = (delta >= 0) & (delta < WINDOW) & (kpos >= 0)
    s = jnp.where(valid[None, :, None, None], s + bias, NEG_INF)
    sink = sinks.astype(jnp.float32).reshape(SWA_KV_HEADS, SWA_GROUP)[None, None, :, :, None, None]
    m = jnp.maximum(jnp.max(s, axis=-1, keepdims=True), sink)
    e = jnp.exp(s - m)
    p = e / (jnp.sum(e, axis=-1, keepdims=True) + jnp.exp(sink - m))
    out = jnp.einsum('bnhgqk,bnkhd->bnqhgd', p.astype(v.dtype), vv).reshape(B, S, SWA_Q_W)
    return out @ w_out


def swiglu(h, w_gate, w_up, w_down):
    return (jax.nn.silu(h @ w_gate) * (h @ w_up)) @ w_down


def setup_inputs(seed: int = 0) -> dict:
    key = jax.random.key(seed)
    ks = iter(jax.random.split(key, 32))

    def nrm(shape, scale):
        return scale * jax.random.normal(next(ks), shape, jnp.float32)

    def gain(shape):
        return 1.0 + nrm(shape, 0.1)

    inv = D_MODEL ** -0.5
    return {
        'x': nrm((BATCH, SEQ, D_MODEL), 1.0),
        'rel_bias_table': nrm((NUM_BUCKETS, REL_HEADS), 0.5),
        'ev_attn_norm': gain((N_EVEN, D_MODEL)),
        'ev_w_in': nrm((N_EVEN, D_MODEL, EVEN_IN_WIDTH), inv),
        'ev_b_forget': 3.0 + nrm((N_EVEN, FOX_HEADS), 0.5),
        'ev_fox_q_norm': gain((N_EVEN, HEAD_DIM)),
        'ev_fox_k_norm': gain((N_EVEN, HEAD_DIM)),
        'ev_diff_q_norm': gain((N_EVEN, HEAD_DIM)),
        'ev_diff_k_norm': gain((N_EVEN, HEAD_DIM)),
        'ev_lambda_q1': nrm((N_EVEN, HEAD_DIM), 0.1),
        'ev_lambda_k1': nrm((N_EVEN, HEAD_DIM), 0.1),
        'ev_lambda_q2': nrm((N_EVEN, HEAD_DIM), 0.1),
        'ev_lambda_k2': nrm((N_EVEN, HEAD_DIM), 0.1),
        'ev_diff_subln': gain((N_EVEN, 2 * HEAD_DIM)),
        'ev_w_out': nrm((N_EVEN, MIX_WIDTH, D_MODEL), MIX_WIDTH ** -0.5),
        'od_attn_norm': gain((N_ODD, D_MODEL)),
        'od_w_qkv': nrm((N_ODD, D_MODEL, SWA_QKV_WIDTH), inv),
        'od_q_norm': gain((N_ODD, HEAD_DIM)),
        'od_k_norm': gain((N_ODD, HEAD_DIM)),
        'od_sinks': nrm((N_ODD, SWA_HEADS), 0.5),
        'od_w_out': nrm((N_ODD, SWA_Q_W, D_MODEL), SWA_Q_W ** -0.5),
        'ffn_norm': gain((DEPTH, D_MODEL)),
        'w_gate': nrm((DEPTH, D_MODEL, FFN_HIDDEN), inv),
        'w_up': nrm((DEPTH, D_MODEL, FFN_HIDDEN), inv),
        'w_down': nrm((DEPTH, FFN_HIDDEN, D_MODEL), FFN_HIDDEN ** -0.5),
    }


def reference(x, rel_bias_table, ev_attn_norm, ev_w_in, ev_b_forget, ev_fox_q_norm, ev_fox_k_norm,
              ev_diff_q_norm, ev_diff_k_norm, ev_lambda_q1, ev_lambda_k1, ev_lambda_q2, ev_lambda_k2,
              ev_diff_subln, ev_w_out, od_attn_norm, od_w_qkv, od_q_norm, od_k_norm, od_sinks, od_w_out,
              ffn_norm, w_gate, w_up, w_down):
    for layer in range(DEPTH):
        j = layer // 2
        if layer % 2 == 0:
            lambda_init = 0.8 - 0.6 * math.exp(-0.3 * layer)
            h = rms_norm(x, ev_attn_norm[j])
            x = x + even_mixer(h, ev_w_in[j], ev_b_forget[j], ev_fox_q_norm[j], ev_fox_k_norm[j],
                               ev_diff_q_norm[j], ev_diff_k_norm[j], ev_lambda_q1[j], ev_lambda_k1[j],
                               ev_lambda_q2[j], ev_lambda_k2[j], ev_diff_subln[j], ev_w_out[j],
                               rel_bias_table, lambda_init)
        else:
            h = rms_norm(x, od_attn_norm[j])
            x = x + odd_mixer(h, od_w_qkv[j], od_q_norm[j], od_k_norm[j], od_sinks[j], od_w_out[j],
                              rel_bias_table)
        h = rms_norm(x, ffn_norm[layer])
        x = x + swiglu(h, w_gate[layer], w_up[layer], w_down[layer])
    return x
```

```python
import numpy as np
from contextlib import ExitStack
import concourse.bass as bass
import concourse.mybir as mybir
from concourse.bass_utils import run_bass_kernel_spmd

F32 = mybir.dt.float32
BF16 = mybir.dt.bfloat16
AF = mybir.ActivationFunctionType
ALU = mybir.AluOpType
AX = mybir.AxisListType

NCORES = 8
S = 2048
D = 1024
NT = S // 128
FF = 2816
NCH = FF // 128
EPS = 1e-6
SCALE = 0.125

O_FQ, O_FK, O_DQ, O_DK, O_OQ, O_OK, O_SUB = 0, 64, 128, 192, 256, 320, 384
O_L = 512
O_BF, O_SINK, O_B31 = 768, 776, 792
NSM = 800


class Buf:
    __slots__ = ("name", "w", "r")

    def __init__(self, name=""):
        self.name = name
        self.w = None
        self.r = {}


class Eng:
    def __init__(self, h, name):
        self.h = h
        self.name = name
        self.sem = None
        self.n = 0
        self.seen = {}
        self.dsems = []
        self.dnext = 0


class MK:
    def __init__(self, nc, es):
        self.nc = nc
        self.semh = {}

        def mksem(name):
            self.semh[name] = es.enter_context(nc.semaphore(name))
            return name

        self.pe = Eng(nc.tensor, "pe")
        self.act = Eng(nc.scalar, "act")
        self.dve = Eng(nc.vector, "dve")
        self.pool = Eng(nc.gpsimd, "pool")
        self.sp = Eng(nc.sync, "sp")
        for e in (self.pe, self.act, self.dve, self.pool):
            e.sem = mksem("s_" + e.name)
        self.sp.dsems = [[mksem(f"d_sp{i}"), 0] for i in range(24)]
        self.pool.dsems = [[mksem(f"d_pl{i}"), 0] for i in range(12)]
        self.nwaits = 0

    @staticmethod
    def _deps(reads, writes):
        deps = {}
        for b in reads:
            if b.w is not None and deps.get(b.w[0], 0) < b.w[1]:
                deps[b.w[0]] = b.w[1]
        for b in writes:
            if b.w is not None and deps.get(b.w[0], 0) < b.w[1]:
                deps[b.w[0]] = b.w[1]
            for k, v in b.r.items():
                if deps.get(k, 0) < v:
                    deps[k] = v
        return deps

    def _wait(self, eng, deps):
        for k, v in deps.items():
            if eng.seen.get(k, 0) >= v:
                continue
            eng.h.wait_ge(self.semh[k], v)
            eng.seen[k] = v
            self.nwaits += 1

    @staticmethod
    def _record(d, reads, writes):
        for b in reads:
            if b.r.get(d[0], 0) < d[1]:
                b.r[d[0]] = d[1]
        for b in writes:
            b.w = d
            b.r = {}

    def op(self, eng, fn, reads=(), writes=()):
        deps = self._deps(reads, writes)
        if eng is self.pe:
            deps.pop(eng.sem, None)
        self._wait(eng, deps)
        ins = fn()
        eng.n += 1
        ins.then_inc(self.semh[eng.sem], 1)
        self._record((eng.sem, eng.n), reads, writes)
        return ins

    def dma(self, q, out, in_, reads=(), writes=()):
        deps = self._deps(reads, writes)
        slot = q.dsems[q.dnext % len(q.dsems)]
        q.dnext += 1
        if slot[1] > 0 and deps.get(slot[0], 0) < slot[1] * 16:
            deps[slot[0]] = slot[1] * 16
        self._wait(q, deps)
        ins = q.h.dma_start(out=out, in_=in_)
        slot[1] += 1
        ins.then_inc(self.semh[slot[0]], 16)
        self._record((slot[0], slot[1] * 16), reads, writes)
        return ins

    def barrier(self):
        engs = [self.pe, self.act, self.dve, self.pool]
        for e in engs:
            deps = {f.sem: f.n for f in engs if f is not e and f.n > 0}
            self._wait(e, deps)

    def finish(self):
        for q in (self.sp, self.pool):
            deps = {s[0]: s[1] * 16 for s in q.dsems if s[1] > 0}
            self._wait(q, deps)
        self._wait(self.sp, {e.sem: e.n for e in (self.pe, self.act, self.dve, self.pool) if e.n > 0})


def build_nc(NSEQ=2, STAGES=4):
    nc = bass.Bass("TRN2", target_bir_lowering=False)

    def dram(name, shape, dt=F32, kind="ExternalInput"):
        return nc.dram_tensor(name, shape, dt, kind=kind).ap()

    x_d = dram("x", [NSEQ, S, D])
    w_in_d = dram("w_in", [D, 3080]).rearrange("(k p) n -> p k n", p=128)
    w_out0_d = dram("w_out0", [D, D]).rearrange("(k p) n -> p k n", p=128)
    w_qkv_d = dram("w_qkv", [D, 1536]).rearrange("(k p) n -> p k n", p=128)
    w_out1_d = dram("w_out1", [D, D]).rearrange("(k p) n -> p k n", p=128)
    w_gate_d = dram("w_gate", [2, D, FF])
    w_up_d = dram("w_up", [2, D, FF])
    w_down_d = dram("w_down", [2, FF, D])
    gains_d = dram("gains", [128, 4, D])
    small_d = dram("small", [128, NSM])
    cpack_d = dram("cpack", [128, 5, 128])
    bm_diff_d = dram("bm_diff", [128, 4, 2, 128])
    bm_swa_d = dram("bm_swa", [128, 16, 2, 128])
    out_d = dram("out", [NSEQ, S, D], kind="ExternalOutput")
    r_d = [dram(f"rscr{i}", [NSEQ, S, D], kind="Internal") for i in range(3)]

    with ExitStack() as es:
        mk = MK(nc, es)
        pe, act, dve, pool, sp = mk.pe, mk.act, mk.dve, mk.pool, mk.sp
        V, A, T, G_ = nc.vector, nc.scalar, nc.tensor, nc.gpsimd

        def sb(name, shape, dt):
            return es.enter_context(nc.sbuf_tensor("sb_" + name, shape, dt))

        P = [es.enter_context(nc.psum_tensor(f"ps{i}", [128, 512], F32)) for i in range(8)]
        PB = [Buf(f"ps{i}") for i in range(8)]

        gainT = sb("gainT", [128, D], F32); gainB = Buf()
        small = sb("small", [128, NSM], F32); smallB = Buf()
        cpack = sb("cpack", [128, 5, 128], F32); cpackB = Buf()
        identb = sb("identb", [128, 128], BF16)
        tri0b = sb("tri0b", [128, 128], BF16)
        onesb = sb("onesb", [128, 128], BF16)
        ehb = sb("ehb", [128, 16], BF16)
        constB = Buf()
        cst = sb("cst", [128, 16], F32)
        hT = sb("hT", [128, 8, S], BF16); hTB = [Buf() for _ in range(NT)]
        R2 = sb("R2", [128, 24704], BF16)
        RW = sb("RW", [128, 32768], BF16)
        xin = [sb(f"xin{i}", [128, D], F32) for i in range(3)]; xinB = [Buf() for _ in range(3)]
        hb = [sb(f"hb{i}", [128, D], BF16) for i in range(2)]; hbB = [Buf() for _ in range(2)]
        sq = [sb(f"sq{i}", [128, 512], F32) for i in range(2)]; sqB = [Buf() for _ in range(2)]
        qn = [sb(f"qn{i}", [128, 512], F32) for i in range(2)]; qnB = [Buf() for _ in range(2)]
        qb = [sb(f"qb{i}", [128, 512], BF16) for i in range(3)]; qbB = [Buf() for _ in range(3)]
        mT = [sb(f"mT{i}", [128, 8, 128], BF16) for i in range(2)]; mTB = [Buf() for _ in range(2)]
        NPT = 4
        PT = [sb(f"PT{i}", [128, 512], BF16) for i in range(NPT)]; PTB = [Buf() for _ in range(NPT)]
        etmp = [sb(f"etmp{i}", [128, 256], F32) for i in range(2)]; etmpB = [Buf() for _ in range(2)]
        sg = [sb(f"sg{i}", [128, 512], BF16) for i in range(2)]; sgB = [Buf() for _ in range(2)]
        qzx = [sb(f"qzx{i}", [128, 512], BF16) for i in range(2)]; qzxB = [Buf() for _ in range(2)]
        qz = [[sg[0], qzx[0]], [sg[1], qzx[1]]]; qzB = [[sgB[0], qzxB[0]], [sgB[1], qzxB[1]]]
        o0buf = sb("o0buf", [128, 4, 128], F32); o0B = [Buf() for _ in range(4)]
        o0flat = o0buf[:, :, :].rearrange("p a b -> p (a b)")
        etmp4 = etmp + [o0flat[:, 0:256], o0flat[:, 256:512]]; etmp4B = etmpB + [Buf(), Buf()]
        dbuf = [sb(f"dbuf{i}", [128, 128], F32) for i in range(4)]; dbufB = [Buf() for _ in range(4)]
        st = sb("st", [128, 32 * 24], F32)
        ring = [(st[:, i * 32:(i + 1) * 32], Buf()) for i in range(24)]
        cpos = sb("cpos", [128, NT, 8], F32); cposB = Buf()
        carry = sb("carry", [128, NT + 1, 8], F32); carryB = Buf()
        biasF = sb("biasF", [128, 4, NT, 8], F32); biasFB = Buf()

        mix = RW[:, 0:16384].rearrange("p (t n) -> p t n", t=NT); mixB = [Buf() for _ in range(NT)]
        wchunk = [RW[:, 16384 + i * 4096:16384 + (i + 1) * 4096].rearrange("p (k n) -> p k n", k=8) for i in range(2)]
        wchunkB = [Buf() for _ in range(2)]
        wout = RW[:, 16384:24576].rearrange("p (k n) -> p k n", k=8); woutB = Buf()
        Ebuf = RW[:, 24576:32768].bitcast(F32).rearrange("p (h e q) -> p h e q", h=16, e=2); EB = Buf()
        wd = RW[:, 0:22528].rearrange("p (c n) -> p c n", c=NCH); wdB = [Buf() for _ in range(2)]
        wgu = [[RW[:, 24576 + (i * 2 + j) * 2048:24576 + (i * 2 + j + 1) * 2048].rearrange("p (k n) -> p k n", k=8)
                for j in range(2)] for i in range(2)]
        wguB = [[Buf() for _ in range(2)] for _ in range(2)]
        qT0 = R2[:, 0:8192].rearrange("p (a n) -> p a n", a=4)
        kT0 = R2[:, 8192:16384].rearrange("p (a n) -> p a n", a=4)
        vF = R2[:, 16384:16384 + 8320].rearrange("p (t h e) -> p t h e", t=NT, h=8)
        vD = R2[:, 16384:16384 + 8256].rearrange("p (t h e) -> p t h e", t=NT, h=4)
        qT1 = R2[:, 0:16384].rearrange("p (a n) -> p a n", a=8)
        kT1 = R2[:, 16384:20480].rearrange("p (a n) -> p a n", a=2)
        vS = R2[:, 20480:20480 + 4160].rearrange("p (t h e) -> p t h e", t=NT, h=4)
        hidT = R2[:, 0:22528].rearrange("p (c n) -> p c n", c=NCH)
        qTB = [Buf() for _ in range(NT)]
        kTB = [Buf() for _ in range(NT)]
        vB = [Buf() for _ in range(NT)]
        vonesB = Buf()
        hidB = [[Buf() for _ in range(2)] for _ in range(NCH)]

        dramB = {}

        def dB(d, s_, t_):
            key = (id(d), s_, t_)
            if key not in dramB:
                dramB[key] = Buf()
            return dramB[key]

        marks = []

        def mark(lbl):
            marks.append((lbl, pe.n, act.n, dve.n))

        state = {"acc": 0, "qz": 0, "sT": 0, "qb": 0, "ring": 0, "tp": 0, "ps": 0, "xin": 0, "w": 0, "pt": 0, "et": 0}

        def smslot():
            r = ring[state["ring"] % len(ring)]
            state["ring"] += 1
            return r

        TPBANKS = [3, 7]

        def tpbank():
            i = TPBANKS[state["tp"] % 2]
            state["tp"] += 1
            return P[i][:, :].bitcast(BF16), PB[i]

        def tpbank_f32():
            i = TPBANKS[state["tp"] % 2]
            state["tp"] += 1
            return P[i], PB[i]

        def psbank():
            i = state["ps"] % 3
            state["ps"] += 1
            return P[i], PB[i]

        def nextxin():
            i = state["xin"] % 3
            state["xin"] += 1
            return xin[i], xinB[i]

        mk.dma(sp, small[:], small_d, writes=[smallB])
        mk.dma(sp, cpack[:], cpack_d, writes=[cpackB])
        mk.op(dve, lambda: V.tensor_copy(out=identb[:], in_=cpack[:, 0, :]), reads=[cpackB], writes=[constB])
        mk.op(dve, lambda: V.tensor_copy(out=tri0b[:], in_=cpack[:, 3, :]), reads=[cpackB], writes=[constB])
        mk.op(dve, lambda: V.memset(cst[:, 0:1], EPS), writes=[constB])
        mk.op(dve, lambda: V.memset(cst[:, 1:2], 1.0), writes=[constB])
        LAMBDA_INIT = 0.8 - 0.6 * 1.0
        lsl, lslB = smslot()
        prod = qn[0]
        mk.op(dve, lambda: V.tensor_tensor(out=prod[:, 0:64], in0=small[:, O_L:O_L + 64], in1=small[:, O_L + 64:O_L + 128], op=ALU.mult),
              reads=[smallB], writes=[qnB[0]])
        mk.op(dve, lambda: V.tensor_tensor(out=prod[:, 64:128], in0=small[:, O_L + 128:O_L + 192], in1=small[:, O_L + 192:O_L + 256], op=ALU.mult),
              reads=[smallB], writes=[qnB[0]])
        mk.op(dve, lambda: V.tensor_reduce(out=lsl[:, 0:2], in_=prod[:, 0:128].rearrange("p (a d) -> p a d", a=2), axis=AX.X, op=ALU.add),
              reads=[qnB[0]], writes=[lslB])
        mk.op(act, lambda: A.activation(out=lsl[:, 2:4], in_=lsl[:, 0:2], func=AF.Exp), reads=[lslB], writes=[lslB])
        mk.op(dve, lambda: V.tensor_tensor(out=lsl[:, 4:5], in0=lsl[:, 3:4], in1=lsl[:, 2:3], op=ALU.subtract), reads=[lslB], writes=[lslB])
        mk.op(dve, lambda: V.tensor_scalar(out=cst[:, 2:3], in0=lsl[:, 4:5], scalar1=-LAMBDA_INIT, scalar2=None, op0=ALU.add),
              reads=[lslB], writes=[constB])
        mk.op(dve, lambda: V.tensor_scalar(out=cst[:, 8:12], in0=small[:, O_B31:O_B31 + 4], scalar1=-1.0, scalar2=None, op0=ALU.mult),
              reads=[smallB], writes=[constB])
        mk.op(act, lambda: A.activation(out=small[:, O_SINK:O_SINK + 16], in_=small[:, O_SINK:O_SINK + 16], func=AF.Exp),
              reads=[smallB], writes=[smallB])
        mk.op(dve, lambda: V.tensor_copy(out=onesb[:], in_=cpack[:, 2, :]), reads=[cpackB], writes=[constB])
        mk.op(dve, lambda: V.tensor_copy(out=ehb[:], in_=small[:, O_SINK:O_SINK + 16]), reads=[smallB], writes=[constB])
        mk.op(dve, lambda: V.tensor_tensor(out=lsl[:, 8:24], in0=small[:, O_SINK:O_SINK + 16], in1=ehb[:], op=ALU.subtract), reads=[smallB, constB], writes=[lslB])
        mk.op(dve, lambda: V.tensor_scalar(out=ehb[0:64, :], in0=ehb[0:64, :], scalar1=1.0 / 64, scalar2=None, op0=ALU.mult), reads=[constB], writes=[constB])
        mk.op(dve, lambda: V.tensor_scalar(out=ehb[64:128, :], in0=lsl[64:128, 8:24], scalar1=1.0 / 64, scalar2=None, op0=ALU.mult), reads=[lslB, constB], writes=[constB])
        mk.op(dve, lambda: V.tensor_scalar(out=small[:, O_SUB:O_SUB + 128], in0=small[:, O_SUB:O_SUB + 128], scalar1=1.0 - LAMBDA_INIT,
                                           scalar2=None, op0=ALU.mult), reads=[smallB], writes=[smallB])
        for (oq, ok) in ((O_FQ, O_FK), (O_DQ, O_DK), (O_OQ, O_OK)):
            mk.op(dve, lambda oq=oq, ok=ok: V.tensor_tensor(out=small[:, ok:ok + 64], in0=small[:, ok:ok + 64], in1=small[:, oq:oq + 64], op=ALU.mult),
                  reads=[smallB], writes=[smallB])
        eps_ap = cst[:, 0:1]
        one_ap = cst[:, 1:2]
        neglam_ap = cst[:, 2:3]

        def rstd_chain(sl, slB, n, inv_n):
            mk.op(act, lambda: A.activation(out=sl[:, 8:8 + n], in_=sl[:, 0:n], func=AF.Ln, scale=inv_n, bias=eps_ap),
                  reads=[slB, constB], writes=[slB])
            mk.op(act, lambda: A.activation(out=sl[:, 16:16 + n], in_=sl[:, 8:8 + n], func=AF.Exp, scale=-0.5),
                  reads=[slB], writes=[slB])
            return sl[:, 16:16 + n]

        def norm_s1(xt, xtB, t):
            b = t % 2
            sl, slB = smslot()
            mk.op(act, lambda: A.activation(out=hb[b][:], in_=xt[:], func=AF.Square, accum_out=sl[:, 0:1]),
                  reads=[xtB], writes=[hbB[b], slB])
            r = rstd_chain(sl, slB, 1, 1.0 / D)
            mk.op(dve, lambda: V.scalar_tensor_tensor(out=hb[b][:], in0=xt[:], scalar=r, in1=gainT[:], op0=ALU.mult, op1=ALU.mult),
                  reads=[xtB, slB, gainB], writes=[hbB[b]])

        def norm_s2(t):
            b = t % 2
            tpv, tpB = tpbank()
            for k in range(8):
                mk.op(pe, lambda k=k: T.transpose(out=tpv[:, k * 128:(k + 1) * 128], in_=hb[b][:, k * 128:(k + 1) * 128], identity=identb[:]),
                      reads=[hbB[b], constB], writes=[tpB])
            mk.op(act, lambda: A.copy(out=hT[:, 0:4, t * 128:(t + 1) * 128], in_=tpv[:, 0:512].rearrange("p (k n) -> p k n", k=4)),
                  reads=[tpB], writes=[hTB[t]])
            mk.op(dve, lambda: V.tensor_copy(out=hT[:, 4:8, t * 128:(t + 1) * 128], in_=tpv[:, 512:1024].rearrange("p (k n) -> p k n", k=4)),
                  reads=[tpB], writes=[hTB[t]])

        def make_norm_pre(src_d, s, gi):
            mk.dma(sp, gainT[:], gains_d[:, gi, :], writes=[gainB])

            def s1(t):
                xt, xtB = nextxin()
                mk.dma(sp, xt[:], src_d[s, t * 128:(t + 1) * 128, :], reads=[dB(src_d, s, t)], writes=[xtB])
                norm_s1(xt, xtB, t)

            def pre(t):
                if t == 0:
                    s1(0)
                    s1(1)
                    norm_s2(0)
                if t + 1 < NT:
                    norm_s2(t + 1)
                if t + 2 < NT:
                    s1(t + 2)
            return pre

        def build_E_swa_head(h):
            mk.op(act, lambda: A.activation(out=Ebuf[:, h], in_=Ebuf[:, h], func=AF.Exp), reads=[EB], writes=[EB])
            mk.op(dve, lambda: V.tensor_tensor(out=Ebuf[:, h], in0=Ebuf[:, h], in1=cpack[:, 3:5, :], op=ALU.mult), reads=[EB, cpackB], writes=[EB])

        def proj_chunks(chunks, pre=None, lag=2, mid=None):
            def issue(c):
                i = state["w"] % 2
                state["w"] += 1
                mk.dma(pool, wchunk[i][:, :, 0:chunks[c][1]], chunks[c][0], writes=[wchunkB[i], wdB[0], wdB[1]])
                return i
            nxt = issue(0)
            pending = []
            for c in range(len(chunks)):
                cur = nxt
                if c + 1 < len(chunks):
                    nxt = issue(c + 1)
                ncols = chunks[c][1]
                for t in range(NT):
                    if c == 0 and pre is not None:
                        pre(t)
                    if c == 1 and mid is not None:
                        mid(t)
                    ps, psB = psbank()
                    for k in range(8):
                        mk.op(pe, lambda k=k: T.matmul(out=ps[:, 0:ncols], lhsT=hT[:, k, t * 128:(t + 1) * 128], rhs=wchunk[cur][:, k, 0:ncols],
                                                       start=(k == 0), stop=(k == 7)),
                              reads=[hTB[t], wchunkB[cur]], writes=[psB])
                    s2 = chunks[c][2](t, ps, psB)
                    if s2 is not None:
                        pending.append(s2)
                    while len(pending) > lag:
                        pending.pop(0)()
            while pending:
                pending.pop(0)()

        def evac_qknorm(gain_off, nheads, dst, dstB, col0=0, pair0=0, perm=False):
            def fn(t, ps, psB):
                b = state["qb"] % 3
                state["qb"] += 1
                n = nheads * 64
                src_ = ps[:, col0:col0 + n]
                sb_ = b % 2
                mk.op(act, lambda: A.activation(out=sq[sb_][:, 0:n], in_=src_, func=AF.Square), reads=[psB], writes=[sqB[sb_]])
                sl, slB = smslot()
                mk.op(dve, lambda: V.tensor_reduce(out=sl[:, 0:nheads], in_=sq[sb_][:, 0:n].rearrange("p (h d) -> p h d", d=64), axis=AX.X, op=ALU.add),
                      reads=[sqB[sb_]], writes=[slB])
                r = rstd_chain(sl, slB, nheads, 1.0 / 64)
                if gain_off is None:
                    if perm:
                        mk.op(dve, lambda: V.tensor_tensor(out=qb[b][:, 0:512].rearrange("p (a f d) -> p f a d", a=4, f=2, d=64),
                                                           in0=src_.rearrange("p (f a d) -> p f a d", f=2, a=4, d=64),
                                                           in1=r.rearrange("p (f a) -> p f a", f=2).unsqueeze(3).broadcast_to([128, 2, 4, 64]), op=ALU.mult),
                              reads=[psB, slB], writes=[qbB[b]])
                    else:
                        mk.op(dve, lambda: V.tensor_tensor(out=qb[b][:, 0:n].rearrange("p (h d) -> p h d", d=64), in0=src_.rearrange("p (h d) -> p h d", d=64),
                                                           in1=r.unsqueeze(2).broadcast_to([128, nheads, 64]), op=ALU.mult),
                              reads=[psB, slB], writes=[qbB[b]])
                else:
                    qi = b % 2
                    mk.op(dve, lambda: V.tensor_tensor(out=qn[qi][:, 0:n].rearrange("p (h d) -> p h d", d=64), in0=src_.rearrange("p (h d) -> p h d", d=64),
                                                       in1=r.unsqueeze(2).broadcast_to([128, nheads, 64]), op=ALU.mult),
                          reads=[psB, slB], writes=[qnB[qi]])
                    mk.op(dve, lambda: V.tensor_tensor(out=qb[b][:, 0:n].rearrange("p (h d) -> p h d", d=64), in0=qn[qi][:, 0:n].rearrange("p (h d) -> p h d", d=64),
                                                       in1=small[:, gain_off:gain_off + 64].unsqueeze(1).broadcast_to([128, nheads, 64]), op=ALU.mult),
                          reads=[qnB[qi], smallB], writes=[qbB[b]])
                npairs = nheads // 2

                def s2():
                    tpv, tpB = tpbank()
                    for pr in range(npairs):
                        mk.op(pe, lambda pr=pr: T.transpose(out=tpv[:, pr * 128:(pr + 1) * 128], in_=qb[b][:, pr * 128:(pr + 1) * 128], identity=identb[:]),
                              reads=[qbB[b], constB], writes=[tpB])
                    mk.op(act, lambda: A.copy(out=dst[:, pair0:pair0 + npairs, t * 128:(t + 1) * 128],
                                              in_=tpv[:, 0:npairs * 128].rearrange("p (a n) -> p a n", a=npairs)),
                          reads=[tpB], writes=[dstB[t]])
                return s2
            return fn

        def evac_v(vview, nheads, hd, col0=0):
            def fn(t, ps, psB):
                mk.op(act, lambda: A.copy(out=vview[:, t, :, 0:hd], in_=ps[:, col0:col0 + nheads * hd].rearrange("p (h d) -> p h d", d=hd)),
                      reads=[psB, vonesB], writes=[vB[t]])
            return fn

        def set_v_ones(vview, hd):
            mk.op(pool, lambda: G_.memset(vview[:, :, :, hd:hd + 1], 1.0), reads=[], writes=[vonesB] + vB)

        def evac_gate(t, ps, psB):
            sl, slB = smslot()
            mk.op(dve, lambda: V.tensor_tensor(out=sl[:, 0:8], in0=ps[:, 0:8], in1=small[:, O_BF:O_BF + 8], op=ALU.add),
                  reads=[psB, smallB], writes=[slB])
            mk.op(act, lambda: A.activation(out=sl[:, 8:16], in_=sl[:, 0:8], func=AF.Exp, scale=-1.0), reads=[slB], writes=[slB])
            mk.op(act, lambda: A.activation(out=sl[:, 16:24], in_=sl[:, 8:16], func=AF.Ln, scale=1.0, bias=one_ap), reads=[slB, constB], writes=[slB])

            def s2():
                cps, cpsB = tpbank_f32()
                mk.op(pe, lambda: T.matmul(out=cps[:, 0:8], lhsT=cpack[:, 1, :], rhs=sl[:, 16:24], start=True, stop=True), reads=[slB, cpackB], writes=[cpsB])
                mk.op(pe, lambda: T.matmul(out=cps[:, 8:16], lhsT=cpack[:, 2, :], rhs=sl[:, 16:24], start=True, stop=True), reads=[slB, cpackB], writes=[cpsB])
                mk.op(dve, lambda: V.tensor_tensor(out=cpos[:, t, :], in0=cps[:, 0:8], in1=carry[:, t, :], op=ALU.add), reads=[cpsB, carryB], writes=[cposB])
                mk.op(dve, lambda: V.tensor_tensor(out=carry[:, t + 1, :], in0=cps[:, 8:16], in1=carry[:, t, :], op=ALU.add), reads=[cpsB, carryB], writes=[carryB])
            return s2

        ACC = [4, 5, 6, 7]

        def sTbank():
            i = state["sT"] % 4
            state["sT"] += 1
            return P[i], PB[i]

        def zero_qz():
            for par in range(2):
                for i in range(2):
                    o = (1 - par) * 64
                    mk.op(pool, lambda par=par, i=i, o=o: G_.memset(qz[par][i][o:o + 64, :], 0.0), writes=[qzB[par][i]])

        def load_qz(par, pr, col0):
            i = state["qz"] % 2
            state["qz"] += 1
            o = par * 64
            t0_ = col0 // 128
            mk.op(pool, lambda: G_.tensor_copy(out=qz[par][i][o:o + 64, :], in_=qT0[o:o + 64, pr, col0:col0 + 512]),
                  reads=[qTB[t0_ + j] for j in range(4)], writes=[qzB[par][i]])
            return qz[par][i], qzB[par][i]

        def attn_stream(groups, qk_fn, step_fn, evac_fn, qz_ahead=6, qk_ahead=2):
            steps = [(g, kb) for g in groups for kb in range(g["nkb"])]
            ns = len(steps)
            qk_done = 0
            qz_done = 0
            late = []
            for si in range(ns):
                while qz_done < ns and qz_done <= si + qz_ahead:
                    g = steps[qz_done][0]
                    if "qzt" not in g:
                        g["qzt"], g["qztB"] = load_qz(g["par"], g["pr"], g["G"] * 512)
                    qz_done += 1
                while qk_done < ns and qk_done <= si + qk_ahead:
                    g, kb = steps[qk_done]
                    g.setdefault("pend", {})[kb] = qk_fn(g, kb)
                    qk_done += 1
                g, kb = steps[si]
                step_fn(g, kb, *g["pend"].pop(kb))
                while late and late[0][0] <= si:
                    late.pop(0)[1]()
                if kb == g["nkb"] - 1:
                    c = evac_fn(g)
                    if c is not None:
                        late.append((si + 8, c))
            while late:
                late.pop(0)[1]()

        def qk_common(g, kb):
            G = g["G"]
            r = kb - 4 * G
            c0 = max(r, 0) * 128
            sT, sTB = sTbank()
            mk.op(pe, lambda: T.matmul(out=sT[:, c0:512], lhsT=kT0[:, g["pr"], kb * 128:(kb + 1) * 128], rhs=g["qzt"][:, c0:512], start=True, stop=True),
                  reads=[kTB[kb], g["qztB"]], writes=[sTB])
            return sT, sTB, c0, r

        def fox_attention():
            for G in range(4):
                mk.op(dve, lambda G=G: V.tensor_tensor(out=biasF[:, G], in0=cpos[:], in1=carry[:, 4 * G + 2, :].unsqueeze(1).broadcast_to([128, NT, 8]),
                                                       op=ALU.subtract), reads=[cposB, carryB], writes=[biasFB])
            groups = []
            for h in range(8):
                for G in range(4):
                    groups.append({"h": h, "G": G, "par": h % 2, "pr": h // 2, "nkb": 4 * G + 4, "a": ACC[len(groups) % 4]})

            def step(g, kb, sT, sTB, c0, r):
                h, G, a = g["h"], g["G"], g["a"]
                pi = state["pt"] % NPT
                state["pt"] += 1
                mk.op(act, lambda: A.activation(out=PT[pi][:, c0:512], in_=sT[:, c0:512], func=AF.Exp, scale=SCALE, bias=biasF[:, G, kb, h:h + 1]),
                      reads=[sTB, biasFB], writes=[PTB[pi]])
                if r >= 0:
                    mk.op(dve, lambda: V.tensor_tensor(out=PT[pi][:, c0:c0 + 128], in0=PT[pi][:, c0:c0 + 128], in1=tri0b[:], op=ALU.mult),
                          reads=[PTB[pi], constB], writes=[PTB[pi]])
                for j in range(max(r, 0), 4):
                    mk.op(pe, lambda j=j: T.matmul(out=P[a][:, j * 65:(j + 1) * 65], lhsT=PT[pi][:, j * 128:(j + 1) * 128], rhs=vF[:, kb, h, :],
                                                   start=(kb == 0 and j == 0), stop=(kb == 4 * G + j), skip_group_check=True),
                          reads=[PTB[pi], vB[kb]], writes=[PB[a]])

            def evac(g):
                h, G, a = g["h"], g["G"], g["a"]
                sl, slB = smslot()
                accv = P[a][:, 0:260].rearrange("p (j e) -> p j e", e=65)
                mk.op(dve, lambda: V.reciprocal(out=sl[:, 0:4], in_=accv[:, :, 64]), reads=[PB[a]], writes=[slB])
                for j in range(4):
                    mk.op(dve, lambda j=j: V.tensor_scalar(out=mix[:, 4 * G + j, h * 64:(h + 1) * 64], in0=P[a][:, j * 65:j * 65 + 64], scalar1=sl[:, j:j + 1],
                                                           scalar2=None, op0=ALU.mult), reads=[PB[a], slB], writes=[mixB[4 * G + j]])

            attn_stream(groups, qk_common, step, evac)

        def load_E(bm_d, nh):
            mk.dma(pool, Ebuf[:, 0:nh], bm_d, writes=[EB] + [wguB[i][j] for i in range(2) for j in range(2)])

        def build_E(bm_d, nh, negb_col, use_tri1):
            for h in range(nh):
                if negb_col is not None:
                    mk.op(act, lambda h=h: A.activation(out=Ebuf[:, h], in_=Ebuf[:, h], func=AF.Exp, bias=cst[:, negb_col + h:negb_col + h + 1], scale=1.0),
                          reads=[EB, constB], writes=[EB])
                else:
                    mk.op(act, lambda h=h: A.activation(out=Ebuf[:, h], in_=Ebuf[:, h], func=AF.Exp), reads=[EB], writes=[EB])
            mk.op(dve, lambda: V.tensor_tensor(out=Ebuf[:, 0:nh, 0, :], in0=Ebuf[:, 0:nh, 0, :], in1=cpack[:, 3, :].unsqueeze(1).broadcast_to([128, nh, 128]),
                                               op=ALU.mult), reads=[EB, cpackB], writes=[EB])
            if use_tri1:
                mk.op(dve, lambda: V.tensor_tensor(out=Ebuf[:, 0:nh, 1, :], in0=Ebuf[:, 0:nh, 1, :], in1=cpack[:, 4, :].unsqueeze(1).broadcast_to([128, nh, 128]),
                                                   op=ALU.mult), reads=[EB, cpackB], writes=[EB])

        def diff_attention():
            groups = []
            for h in range(4):
                for G in range(4):
                    for m in range(2):
                        k2 = (len(groups) % 2) * 2
                        groups.append({"h": h, "G": G, "m": m, "par": m, "pr": h, "nkb": 4 * G + 4, "accs": [ACC[k2], ACC[k2 + 1]]})

            def step(g, kb, sT, sTB, c0, r):
                h, G, accs = g["h"], g["G"], g["accs"]
                Eh = Ebuf[:, h].rearrange("p e q -> p (e q)")
                pi = state["pt"] % NPT
                state["pt"] += 1
                far_lo = max(r + 2, 0)
                near_lo, near_hi = max(r, 0), min(r + 1, 3)
                bias_ap = small[:, O_B31 + h:O_B31 + h + 1]
                if far_lo <= 3:
                    f0 = far_lo * 128
                    mk.op(act, lambda: A.activation(out=PT[pi][:, f0:512], in_=sT[:, f0:512], func=AF.Exp, scale=SCALE, bias=bias_ap),
                          reads=[sTB, smallB], writes=[PTB[pi]])
                if near_hi >= near_lo:
                    n0, n1 = near_lo * 128, (near_hi + 1) * 128
                    e0 = (near_lo - r) * 128
                    ei = state["et"] % 2
                    state["et"] += 1
                    mk.op(act, lambda: A.activation(out=etmp[ei][:, 0:n1 - n0], in_=sT[:, n0:n1], func=AF.Exp, scale=SCALE, bias=bias_ap),
                          reads=[sTB, smallB], writes=[etmpB[ei]])
                    mk.op(dve, lambda: V.tensor_tensor(out=PT[pi][:, n0:n1], in0=etmp[ei][:, 0:n1 - n0], in1=Eh[:, e0:e0 + n1 - n0], op=ALU.mult),
                          reads=[etmpB[ei], EB], writes=[PTB[pi]])
                for j in range(max(r, 0), 4):
                    a = accs[j // 2]
                    jo = (j % 2) * 129
                    mk.op(pe, lambda a=a, jo=jo, j=j: T.matmul(out=P[a][:, jo:jo + 129], lhsT=PT[pi][:, j * 128:(j + 1) * 128], rhs=vD[:, kb, h, :],
                                                             start=(kb == 0 and j % 2 == 0), stop=(kb == 4 * G + j), skip_group_check=True),
                          reads=[PTB[pi], vB[kb]], writes=[PB[a]])

            def evac(g):
                h, G, m, accs = g["h"], g["G"], g["m"], g["accs"]
                sl, slB = smslot()
                for jj in range(2):
                    a = accs[jj]
                    mk.op(dve, lambda a=a, jj=jj: V.reciprocal(out=sl[:, 2 * jj:2 * jj + 2], in_=P[a][:, 0:258].rearrange("p (j e) -> p j e", e=129)[:, :, 128]),
                          reads=[PB[a]], writes=[slB])
                if m == 0:
                    for j in range(4):
                        a = accs[j // 2]
                        jo = (j % 2) * 129
                        mk.op(dve, lambda a=a, j=j, jo=jo: V.tensor_scalar(out=o0buf[:, j, :], in0=P[a][:, jo:jo + 128], scalar1=sl[:, j:j + 1], scalar2=None, op0=ALU.mult),
                              reads=[PB[a], slB], writes=[o0B[j]])
                else:
                    mk.op(dve, lambda: V.tensor_scalar(out=sl[:, 4:8], in0=sl[:, 0:4], scalar1=neglam_ap, scalar2=None, op0=ALU.mult), reads=[slB, constB], writes=[slB])
                    s2, s2B = smslot()
                    for j in range(4):
                        a = accs[j // 2]
                        jo = (j % 2) * 129
                        mk.op(dve, lambda a=a, j=j, jo=jo: V.scalar_tensor_tensor(out=dbuf[j][:], in0=P[a][:, jo:jo + 128], scalar=sl[:, 4 + j:5 + j], in1=o0buf[:, j, :],
                                                                                op0=ALU.mult, op1=ALU.add),
                              reads=[PB[a], slB, o0B[j]], writes=[dbufB[j]])
                        bq = j % 2
                        mk.op(dve, lambda j=j, bq=bq: V.scalar_tensor_tensor(out=sq[bq][:, 0:128], in0=dbuf[j][:], scalar=1.0, in1=dbuf[j][:],
                                                                           op0=ALU.mult, op1=ALU.mult, accum_out=s2[:, j:j + 1]),
                              reads=[dbufB[j]], writes=[sqB[bq], s2B])

                    def fin():
                        r2 = rstd_chain(s2, s2B, 4, 1.0 / 128)
                        for j in range(4):
                            mk.op(dve, lambda j=j: V.scalar_tensor_tensor(out=mix[:, 4 * G + j, 512 + h * 128:512 + (h + 1) * 128], in0=dbuf[j][:], scalar=r2[:, j:j + 1],
                                                                        in1=small[:, O_SUB:O_SUB + 128], op0=ALU.mult, op1=ALU.mult),
                                  reads=[dbufB[j], s2B, smallB], writes=[mixB[4 * G + j]])
                    return fin

            attn_stream(groups, qk_common, step, evac)

        def swa_attention():
            heads = []
            for hq in range(16):
                kvh = hq // 4
                cq, hl = hq // 8, hq % 8
                pr_q, pbq = 4 * cq + hl % 4, (hl // 4) * 64
                pr_k, pbk = kvh // 2, (kvh % 2) * 64
                assert pbq == pbk and pr_k == cq
                heads.append({"hq": hq, "kvh": kvh, "pr_q": pr_q, "pr_k": pr_k, "pbk": pbk, "pend": {}})
            steps = [(h, kb) for h in heads for kb in range(NT)]

            def qk(h, kb):
                pbk, pr_k = h["pbk"], h["pr_k"]
                if h["hq"] % 4 == 0 and kb == 0:
                    for hbi in range(2):
                        mk.op(pool, lambda hbi=hbi: G_.memset(hb[hbi][64 - pbk:128 - pbk, :], 0.0), writes=[hbB[hbi]])
                        mk.op(pool, lambda hbi=hbi: G_.tensor_copy(out=hb[hbi][pbk:pbk + 64, :], in_=kT1[pbk:pbk + 64, pr_k, hbi * 1024:(hbi + 1) * 1024]),
                              reads=[kTB[hbi * 8 + j] for j in range(8)], writes=[hbB[hbi]])
                ncols = 256 if kb + 1 < NT else 128
                sT, sTB = sTbank()
                mk.op(pe, lambda: T.matmul(out=sT[:, 0:ncols], lhsT=hb[kb // 8][:, (kb % 8) * 128:(kb % 8 + 1) * 128],
                                           rhs=qT1[:, h["pr_q"], kb * 128:kb * 128 + ncols], start=True, stop=True),
                      reads=[hbB[kb // 8], qTB[kb]] + ([qTB[kb + 1]] if kb + 1 < NT else []), writes=[sTB])
                h["pend"][kb] = (sT, sTB, ncols)

            def accof(t):
                return ACC[(t // 2) % 4], (t % 2) * 65

            def evac2(hq, t0_):
                a, _ = accof(t0_)
                sl, slB = smslot()
                av = P[a][:, 0:130].rearrange("p (j e) -> p j e", e=65)
                mk.op(dve, lambda: V.reciprocal(out=sl[:, 0:2], in_=av[:, :, 64]), reads=[PB[a]], writes=[slB])
                mk.op(dve, lambda: V.tensor_tensor(out=mix[:, t0_:t0_ + 2, hq * 64:(hq + 1) * 64], in0=av[:, :, 0:64],
                                                   in1=sl[:, 0:2].unsqueeze(2).broadcast_to([128, 2, 64]), op=ALU.mult),
                      reads=[PB[a], slB], writes=[mixB[t0_], mixB[t0_ + 1]])

            qk_done = 0
            for si, (h, kb) in enumerate(steps):
                while qk_done < len(steps) and qk_done <= si + 3:
                    qk(*steps[qk_done])
                    qk_done += 1
                hq, kvh = h["hq"], h["kvh"]
                Eh = Ebuf[:, hq].rearrange("p e q -> p (e q)")
                sT, sTB, ncols = h["pend"].pop(kb)
                pi = state["pt"] % NPT
                state["pt"] += 1
                ei = state["et"] % 4
                state["et"] += 1
                mk.op(act, lambda: A.activation(out=etmp4[ei][:, 0:ncols], in_=sT[:, 0:ncols], func=AF.Exp, scale=SCALE), reads=[sTB], writes=[etmp4B[ei]])
                if kb % 2 == 0:
                    mk.op(dve, lambda: V.tensor_tensor(out=PT[pi][:, 0:ncols], in0=etmp4[ei][:, 0:ncols], in1=Eh[:, 0:ncols], op=ALU.mult),
                          reads=[etmp4B[ei], EB], writes=[PTB[pi]])
                else:
                    mk.op(pool, lambda: G_.tensor_tensor(out=PT[pi][:, 0:ncols], in0=etmp4[ei][:, 0:ncols], in1=Eh[:, 0:ncols], op=ALU.mult),
                          reads=[etmp4B[ei], EB], writes=[PTB[pi]])
                a, ao = accof(kb)
                mk.op(pe, lambda: T.matmul(out=P[a][:, ao:ao + 65], lhsT=PT[pi][:, 0:128], rhs=vS[:, kb, kvh, :], start=(kb == 0), stop=False, skip_group_check=True),
                      reads=[PTB[pi], vB[kb]], writes=[PB[a]])
                mk.op(pe, lambda: T.matmul(out=P[a][:, ao + 64:ao + 65], lhsT=onesb[:], rhs=ehb[:, hq:hq + 1], start=False, stop=True, skip_group_check=True),
                      reads=[constB], writes=[PB[a]])
                if kb + 1 < NT:
                    a2, ao2 = accof(kb + 1)
                    mk.op(pe, lambda: T.matmul(out=P[a2][:, ao2:ao2 + 65], lhsT=PT[pi][:, 128:256], rhs=vS[:, kb, kvh, :], start=((kb + 1) % 2 == 0), stop=False,
                                               skip_group_check=True),
                          reads=[PTB[pi], vB[kb]], writes=[PB[a2]])
                if kb >= 2 and kb % 2 == 0:
                    evac2(hq, kb - 2)
                if kb == NT - 1:
                    evac2(hq, NT - 2)

        def phase_outproj(w_d, src_d, dst_d, s, gi):
            mk.dma(pool, wout[:, :, 0:512], w_d[:, :, 0:512], writes=[woutB, wchunkB[0], wchunkB[1]])
            mk.dma(pool, wout[:, :, 512:1024], w_d[:, :, 512:1024], writes=[woutB, wchunkB[0], wchunkB[1]])
            mk.dma(sp, gainT[:], gains_d[:, gi, :], writes=[gainB])
            xts = {}

            def stT(t):
                b = t % 2
                tpv, tpB = tpbank()
                for k in range(8):
                    mk.op(pe, lambda k=k: T.transpose(out=tpv[:, k * 128:(k + 1) * 128], in_=mix[:, t, k * 128:(k + 1) * 128], identity=identb[:]),
                          reads=[mixB[t], constB], writes=[tpB])
                mk.op(act, lambda: A.copy(out=mT[b][:, 0:4, :], in_=tpv[:, 0:512].rearrange("p (k n) -> p k n", k=4)), reads=[tpB], writes=[mTB[b]])
                mk.op(dve, lambda: V.tensor_copy(out=mT[b][:, 4:8, :], in_=tpv[:, 512:1024].rearrange("p (k n) -> p k n", k=4)), reads=[tpB], writes=[mTB[b]])
                xt, xtB = nextxin()
                mk.dma(sp, xt[:], src_d[s, t * 128:(t + 1) * 128, :], reads=[dB(src_d, s, t)], writes=[xtB])
                xts[t] = (xt, xtB)

            def stM(t):
                b = t % 2
                xt, xtB = xts[t]
                for half in range(2):
                    ps, psB = psbank()
                    for k in range(8):
                        mk.op(pe, lambda k=k: T.matmul(out=ps[:, :], lhsT=mT[b][:, k, :], rhs=wout[:, k, half * 512:(half + 1) * 512], start=(k == 0), stop=(k == 7)),
                              reads=[mTB[b], woutB], writes=[psB])
                    mk.op(dve, lambda: V.tensor_tensor(out=xt[:, half * 512:(half + 1) * 512], in0=ps[:, :], in1=xt[:, half * 512:(half + 1) * 512], op=ALU.add),
                          reads=[psB, xtB], writes=[xtB])
                mk.dma(sp, dst_d[s, t * 128:(t + 1) * 128, :], xt[:], reads=[xtB], writes=[dB(dst_d, s, t)])
                norm_s1(xt, xtB, t)

            stT(0)
            for t in range(NT):
                if t + 1 < NT:
                    stT(t + 1)
                stM(t)
                if t >= 1:
                    norm_s2(t - 1)
            norm_s2(NT - 1)

        def phase_ffn(layer, src_d, dst_d, s):
            wg_d = w_gate_d[layer].rearrange("(k p) n -> p k n", p=128)
            wu_d = w_up_d[layer].rearrange("(k p) n -> p k n", p=128)
            wd_d = w_down_d[layer].rearrange("(c p) n -> p c n", p=128)
            blocks = [(i * 256, 256) for i in range(11)]

            def issue_gu(bi):
                i = state["w"] % 2
                state["w"] += 1
                c0, n = blocks[bi]
                mk.dma(pool, wgu[i][0][:, :, 0:n], wg_d[:, :, c0:c0 + n], writes=[wguB[i][0], EB])
                mk.dma(pool, wgu[i][1][:, :, 0:n], wu_d[:, :, c0:c0 + n], writes=[wguB[i][1], EB])
                return i

            for hh in range(2):
                nxt = issue_gu(0)
                wd_parts = [(half, cc) for half in range(2) for cc in range(2)] if hh == 0 else []
                for bi in range(len(blocks)):
                    cur = nxt
                    if bi + 1 < len(blocks):
                        nxt = issue_gu(bi + 1)
                    if bi >= 1 and wd_parts:
                        half, cc = wd_parts.pop(0)
                        mk.dma(pool, wd[:, cc * 11:(cc + 1) * 11, half * 512:(half + 1) * 512], wd_d[:, cc * 11:(cc + 1) * 11, half * 512:(half + 1) * 512],
                               writes=[wdB[half], woutB, wchunkB[0], wchunkB[1]] + mixB)
                    for j in range(blocks[bi][1] // 128):
                        c = blocks[bi][0] // 128 + j
                        for tg in range(2):
                            tok0 = hh * 1024 + tg * 512
                            gi_ = (c * 2 + tg) % 2
                            gps, gB = P[gi_], PB[gi_]
                            ups, uB = P[2 + gi_], PB[2 + gi_]
                            rd = [hTB[tok0 // 128 + q] for q in range(4)]
                            for k in range(8):
                                mk.op(pe, lambda k=k: T.matmul(out=gps[:, :], lhsT=wgu[cur][0][:, k, j * 128:(j + 1) * 128], rhs=hT[:, k, tok0:tok0 + 512],
                                                               start=(k == 0), stop=(k == 7)), reads=rd + [wguB[cur][0]], writes=[gB])
                            for k in range(8):
                                mk.op(pe, lambda k=k: T.matmul(out=ups[:, :], lhsT=wgu[cur][1][:, k, j * 128:(j + 1) * 128], rhs=hT[:, k, tok0:tok0 + 512],
                                                               start=(k == 0), stop=(k == 7)), reads=rd + [wguB[cur][1]], writes=[uB])
                            mk.op(act, lambda: A.activation(out=sg[gi_][:], in_=gps[:, :], func=AF.Silu), reads=[gB], writes=[sgB[gi_]])
                            mk.op(dve, lambda: V.tensor_tensor(out=hidT[:, c, tg * 512:(tg + 1) * 512], in0=ups[:, :], in1=sg[gi_][:], op=ALU.mult),
                                  reads=[uB, sgB[gi_]], writes=[hidB[c][tg]])
                for tt in range(8):
                    t = hh * 8 + tt
                    xt, xtB = nextxin()
                    mk.dma(sp, xt[:], src_d[s, t * 128:(t + 1) * 128, :], reads=[dB(src_d, s, t)], writes=[xtB])
                    for half in range(2):
                        a = ACC[(tt * 2 + half) % 4]
                        for c in range(NCH):
                            mk.op(pe, lambda c=c, a=a: T.matmul(out=P[a][:, :], lhsT=hidT[:, c, tt * 128:(tt + 1) * 128], rhs=wd[:, c, half * 512:(half + 1) * 512],
                                                                start=(c == 0), stop=(c == NCH - 1)),
                                  reads=[hidB[c][tt // 4], wdB[half]], writes=[PB[a]])
                        mk.op(dve, lambda a=a: V.tensor_tensor(out=xt[:, half * 512:(half + 1) * 512], in0=P[a][:, :], in1=xt[:, half * 512:(half + 1) * 512], op=ALU.add),
                              reads=[PB[a], xtB], writes=[xtB])
                    mk.dma(sp, dst_d[s, t * 128:(t + 1) * 128, :], xt[:], reads=[xtB], writes=[dB(dst_d, s, t)])

        allhid = [hidB[c][tg] for c in range(NCH) for tg in range(2)]
        r2bufs = qTB + kTB + vB + [vonesB]

        def fence(eng, bufs):
            if eng is dve:
                mk.op(dve, lambda: V.memset(cst[:, 12:13], 0.0), writes=list(bufs))
            elif eng is pool:
                mk.op(pool, lambda: G_.memset(cst[:, 13:14], 0.0), writes=list(bufs))
            else:
                mk.op(act, lambda: A.copy(out=cst[:, 14:15], in_=cst[:, 1:2]), reads=[constB], writes=list(bufs))

        def fence_before_ffn():
            fence(dve, r2bufs)

        def fence_after_ffn():
            fence(act, allhid)
            fence(pool, allhid)
            fence(dve, allhid + [wdB[0], wdB[1]])

        def zero_carry():
            mk.op(dve, lambda: V.memset(carry[:, 0, :], 0.0), writes=[carryB])

        def dst_of(stage):
            return out_d if stage == STAGES else r_d[stage - 1]

        for s in range(NSEQ):
            src = x_d
            mark("projF"); set_v_ones(vF, 64)
            load_E(bm_diff_d, 4)
            zero_carry()
            proj_chunks(pre=make_norm_pre(src, s, 0), chunks=[
                (w_in_d[:, :, 0:512], 512, evac_qknorm(None, 8, qT0, qTB)),
                (w_in_d[:, :, 512:1024], 512, evac_qknorm(O_FK, 8, kT0, kTB)),
                (w_in_d[:, :, 1024:1536], 512, evac_v(vF, 8, 64)),
                (w_in_d[:, :, 1536:1544], 8, evac_gate),
            ])
            mark("fox"); zero_qz(); build_E(bm_diff_d, 4, 8, False); fox_attention()
            mark("projD"); set_v_ones(vD, 128)
            proj_chunks([
                (w_in_d[:, :, 1544:2056], 512, evac_qknorm(None, 8, qT0, qTB)),
                (w_in_d[:, :, 2056:2568], 512, evac_qknorm(O_DK, 8, kT0, kTB)),
                (w_in_d[:, :, 2568:3080], 512, evac_v(vD, 4, 128)),
            ])
            mark("diff"); diff_attention()
            mark("out0"); phase_outproj(w_out0_d, src, dst_of(1), s, 2)
            if STAGES >= 2:
                fence_before_ffn()
                mark("ffn0"); phase_ffn(0, dst_of(1), dst_of(2), s)
                fence_after_ffn()
            if STAGES >= 3:
                src = dst_of(2)
                set_v_ones(vS, 64)
                load_E(bm_swa_d, 16)
                kfn = evac_qknorm(O_OK, 4, kT1, kTB)
                vfn = evac_v(vS, 4, 64, col0=256)

                def kv_evac(t, ps, psB):
                    s2 = kfn(t, ps, psB)
                    vfn(t, ps, psB)
                    return s2
                mark("projS")
                proj_chunks(pre=make_norm_pre(src, s, 1), mid=build_E_swa_head, chunks=[
                    (w_qkv_d[:, :, 0:512], 512, evac_qknorm(None, 8, qT1, qTB, pair0=0, perm=True)),
                    (w_qkv_d[:, :, 512:1024], 512, evac_qknorm(None, 8, qT1, qTB, pair0=4, perm=True)),
                    (w_qkv_d[:, :, 1024:1536], 512, kv_evac),
                ])
                mark("swa"); swa_attention()
                mark("out1"); phase_outproj(w_out1_d, src, dst_of(3), s, 3)
            if STAGES >= 4:
                fence_before_ffn()
                mark("ffn1"); phase_ffn(1, dst_of(3), dst_of(4), s)
                fence_after_ffn()
            elif STAGES == 3 or STAGES == 1:
                mk.barrier()
        mark("end")
        mk.finish()
        build_nc.marks = marks
        build_nc.stats = {e.name: e.n for e in (pe, act, dve, pool)}
        build_nc.stats["waits"] = mk.nwaits
    return nc


def _t5_bucket(delta):
    n = np.maximum(delta, 0)
    nf = np.maximum(n, 1).astype(np.float32)
    large = 16 + (np.log(nf / np.float32(16)) / np.float32(np.log(8.0)) * np.float32(16)).astype(np.int32)
    large = np.minimum(large, 31)
    return np.where(n < 16, n, large)


def host_prep(inputs):
    f = lambda a: np.ascontiguousarray(np.asarray(a, dtype=np.float32))
    table = f(inputs["rel_bias_table"])
    gains = np.stack([f(inputs["ev_attn_norm"])[0], f(inputs["od_attn_norm"])[0], f(inputs["ffn_norm"])[0], f(inputs["ffn_norm"])[1]])
    gains = np.ascontiguousarray(np.broadcast_to(gains[None], (128, 4, D)))
    sm = np.zeros(NSM, np.float32)
    sm[O_FQ:O_FQ + 64] = f(inputs["ev_fox_q_norm"])[0]
    sm[O_FK:O_FK + 64] = f(inputs["ev_fox_k_norm"])[0]
    sm[O_DQ:O_DQ + 64] = f(inputs["ev_diff_q_norm"])[0]
    sm[O_DK:O_DK + 64] = f(inputs["ev_diff_k_norm"])[0]
    sm[O_OQ:O_OQ + 64] = f(inputs["od_q_norm"])[0]
    sm[O_OK:O_OK + 64] = f(inputs["od_k_norm"])[0]
    sm[O_SUB:O_SUB + 128] = f(inputs["ev_diff_subln"])[0]
    sm[O_L:O_L + 64] = f(inputs["ev_lambda_q1"])[0]
    sm[O_L + 64:O_L + 128] = f(inputs["ev_lambda_k1"])[0]
    sm[O_L + 128:O_L + 192] = f(inputs["ev_lambda_q2"])[0]
    sm[O_L + 192:O_L + 256] = f(inputs["ev_lambda_k2"])[0]
    sm[O_BF:O_BF + 8] = f(inputs["ev_b_forget"])[0]
    sm[O_SINK:O_SINK + 16] = f(inputs["od_sinks"])[0]
    sm[O_B31:O_B31 + 4] = table[31, 0:4]
    small = np.ascontiguousarray(np.broadcast_to(sm[None], (128, NSM)))
    k = np.arange(128)[:, None]
    q = np.arange(128)[None, :]
    cpack = np.zeros((128, 5, 128), np.float32)
    cpack[:, 0] = np.eye(128, dtype=np.float32)
    cpack[:, 1] = (k <= q)
    cpack[:, 2] = 1.0
    cpack[:, 3] = (k <= q)
    cpack[:, 4] = (k > q)
    idx = np.stack([_t5_bucket(q - k), _t5_bucket(128 + q - k)], axis=0)
    bm = table[idx]
    bm = np.ascontiguousarray(np.transpose(bm, (1, 3, 0, 2)))
    common = {
        "w_in": f(inputs["ev_w_in"])[0], "w_out0": f(inputs["ev_w_out"])[0],
        "w_qkv": f(inputs["od_w_qkv"])[0], "w_out1": f(inputs["od_w_out"])[0],
        "w_gate": f(inputs["w_gate"]), "w_up": f(inputs["w_up"]), "w_down": f(inputs["w_down"]),
        "gains": gains, "small": small, "cpack": cpack,
        "bm_diff": np.ascontiguousarray(bm[:, 0:4]), "bm_swa": bm,
    }
    return common


def kernel(**inputs):
    x = np.ascontiguousarray(np.asarray(inputs["x"], dtype=np.float32))
    common = host_prep(inputs)
    per = x.shape[0] // NCORES
    nc = build_nc(NSEQ=per, STAGES=4)
    in_maps = []
    for c in range(NCORES):
        m = dict(common)
        m["x"] = np.ascontiguousarray(x[c * per:(c + 1) * per])
        in_maps.append(m)
    res = run_bass_kernel_spmd(nc, in_maps, core_ids=list(range(NCORES)))
    return np.concatenate([np.asarray(r["out"], dtype=np.float32) for r in res.results], axis=0)
```

```python
import numpy as np
from contextlib import ExitStack
import concourse.bass as bass
import concourse.mybir as mybir
from concourse.bass_utils import run_bass_kernel_spmd

F32 = mybir.dt.float32
BF16 = mybir.dt.bfloat16
AF = mybir.ActivationFunctionType
ALU = mybir.AluOpType
AX = mybir.AxisListType

NCORES = 8
S = 2048
D = 1024
NT = S // 128
FF = 2816
NCH = FF // 128
EPS = 1e-6
SCALE = 0.125

O_FQ, O_FK, O_DQ, O_DK, O_OQ, O_OK, O_SUB = 0, 64, 128, 192, 256, 320, 384
O_L = 512
O_BF, O_SINK, O_B31 = 768, 776, 792
NSM = 800


class Buf:
    __slots__ = ("name", "w", "r")

    def __init__(self, name=""):
        self.name = name
        self.w = None
        self.r = {}


class Eng:
    def __init__(self, h, name):
        self.h = h
        self.name = name
        self.sem = None
        self.n = 0
        self.seen = {}
        self.dsems = []
        self.dnext = 0


class MK:
    def __init__(self, nc, es):
        self.nc = nc
        self.semh = {}

        def mksem(name):
            self.semh[name] = es.enter_context(nc.semaphore(name))
            return name

        self.pe = Eng(nc.tensor, "pe")
        self.act = Eng(nc.scalar, "act")
        self.dve = Eng(nc.vector, "dve")
        self.pool = Eng(nc.gpsimd, "pool")
        self.sp = Eng(nc.sync, "sp")
        for e in (self.pe, self.act, self.dve, self.pool):
            e.sem = mksem("s_" + e.name)
        self.sp.dsems = [[mksem(f"d_sp{i}"), 0] for i in range(24)]
        self.pool.dsems = [[mksem(f"d_pl{i}"), 0] for i in range(6)]
        self.nwaits = 0

    @staticmethod
    def _deps(reads, writes):
        deps = {}
        for b in reads:
            if b.w is not None and deps.get(b.w[0], 0) < b.w[1]:
                deps[b.w[0]] = b.w[1]
        for b in writes:
            if b.w is not None and deps.get(b.w[0], 0) < b.w[1]:
                deps[b.w[0]] = b.w[1]
            for k, v in b.r.items():
                if deps.get(k, 0) < v:
                    deps[k] = v
        return deps

    def _wait(self, eng, deps):
        for k, v in deps.items():
            if eng.seen.get(k, 0) >= v:
                continue
            eng.h.wait_ge(self.semh[k], v)
            eng.seen[k] = v
            self.nwaits += 1

    @staticmethod
    def _record(d, reads, writes):
        for b in reads:
            if b.r.get(d[0], 0) < d[1]:
                b.r[d[0]] = d[1]
        for b in writes:
            b.w = d
            b.r = {}

    def op(self, eng, fn, reads=(), writes=()):
        deps = self._deps(reads, writes)
        if eng is self.pe:
            deps.pop(eng.sem, None)
        self._wait(eng, deps)
        ins = fn()
        eng.n += 1
        ins.then_inc(self.semh[eng.sem], 1)
        self._record((eng.sem, eng.n), reads, writes)
        return ins

    def dma(self, q, out, in_, reads=(), writes=()):
        deps = self._deps(reads, writes)
        slot = q.dsems[q.dnext % len(q.dsems)]
        q.dnext += 1
        if slot[1] > 0 and deps.get(slot[0], 0) < slot[1] * 16:
            deps[slot[0]] = slot[1] * 16
        self._wait(q, deps)
        ins = q.h.dma_start(out=out, in_=in_)
        slot[1] += 1
        ins.then_inc(self.semh[slot[0]], 16)
        self._record((slot[0], slot[1] * 16), reads, writes)
        return ins

    def barrier(self):
        engs = [self.pe, self.act, self.dve, self.pool]
        for e in engs:
            deps = {f.sem: f.n for f in engs if f is not e and f.n > 0}
            self._wait(e, deps)

    def finish(self):
        for q in (self.sp, self.pool):
            deps = {s[0]: s[1] * 16 for s in q.dsems if s[1] > 0}
            self._wait(q, deps)
        self._wait(self.sp, {e.sem: e.n for e in (self.pe, self.act, self.dve, self.pool) if e.n > 0})


def build_nc(NSEQ=2, STAGES=4):
    nc = bass.Bass("TRN2", target_bir_lowering=False)

    def dram(name, shape, dt=F32, kind="ExternalInput"):
        return nc.dram_tensor(name, shape, dt, kind=kind).ap()

    x_d = dram("x", [NSEQ, S, D])
    w_in_d = dram("w_in", [D, 3080]).rearrange("(k p) n -> p k n", p=128)
    w_out0_d = dram("w_out0", [D, D]).rearrange("(k p) n -> p k n", p=128)
    w_qkv_d = dram("w_qkv", [D, 1536]).rearrange("(k p) n -> p k n", p=128)
    w_out1_d = dram("w_out1", [D, D]).rearrange("(k p) n -> p k n", p=128)
    w_gate_d = dram("w_gate", [2, D, FF])
    w_up_d = dram("w_up", [2, D, FF])
    w_down_d = dram("w_down", [2, FF, D])
    gains_d = dram("gains", [128, 4, D])
    small_d = dram("small", [128, NSM])
    cpack_d = dram("cpack", [128, 5, 128])
    bm_diff_d = dram("bm_diff", [128, 4, 2, 128])
    bm_swa_d = dram("bm_swa", [128, 16, 2, 128])
    out_d = dram("out", [NSEQ, S, D], kind="ExternalOutput")
    r_d = [dram(f"rscr{i}", [NSEQ, S, D], kind="Internal") for i in range(3)]

    with ExitStack() as es:
        mk = MK(nc, es)
        pe, act, dve, pool, sp = mk.pe, mk.act, mk.dve, mk.pool, mk.sp
        V, A, T, G_ = nc.vector, nc.scalar, nc.tensor, nc.gpsimd

        def sb(name, shape, dt):
            return es.enter_context(nc.sbuf_tensor("sb_" + name, shape, dt))

        P = [es.enter_context(nc.psum_tensor(f"ps{i}", [128, 512], F32)) for i in range(8)]
        PB = [Buf(f"ps{i}") for i in range(8)]

        gainT = sb("gainT", [128, D], F32); gainB = Buf()
        small = sb("small", [128, NSM], F32); smallB = Buf()
        cpack = sb("cpack", [128, 5, 128], F32); cpackB = Buf()
        identb = sb("identb", [128, 128], BF16)
        tri0b = sb("tri0b", [128, 128], BF16)
        onesb = sb("onesb", [128, 128], BF16)
        ehb = sb("ehb", [128, 16], BF16)
        constB = Buf()
        cst = sb("cst", [128, 16], F32)
        hT = sb("hT", [128, 8, S], BF16); hTB = [Buf() for _ in range(NT)]
        R2 = sb("R2", [128, 24704], BF16)
        RW = sb("RW", [128, 32768], BF16)
        xin = [sb(f"xin{i}", [128, D], F32) for i in range(3)]; xinB = [Buf() for _ in range(3)]
        hb = [sb(f"hb{i}", [128, D], BF16) for i in range(2)]; hbB = [Buf() for _ in range(2)]
        sq = [sb(f"sq{i}", [128, 512], F32) for i in range(2)]; sqB = [Buf() for _ in range(2)]
        qn = [sb(f"qn{i}", [128, 512], F32) for i in range(2)]; qnB = [Buf() for _ in range(2)]
        qb = [sb(f"qb{i}", [128, 512], BF16) for i in range(3)]; qbB = [Buf() for _ in range(3)]
        mT = [sb(f"mT{i}", [128, 8, 128], BF16) for i in range(2)]; mTB = [Buf() for _ in range(2)]
        NPT = 4
        PT = [sb(f"PT{i}", [128, 512], BF16) for i in range(NPT)]; PTB = [Buf() for _ in range(NPT)]
        etmp = [sb(f"etmp{i}", [128, 256], F32) for i in range(2)]; etmpB = [Buf() for _ in range(2)]
        sg = [sb(f"sg{i}", [128, 512], BF16) for i in range(2)]; sgB = [Buf() for _ in range(2)]
        qzx = [sb(f"qzx{i}", [128, 512], BF16) for i in range(2)]; qzxB = [Buf() for _ in range(2)]
        qz = [[sg[0], qzx[0]], [sg[1], qzx[1]]]; qzB = [[sgB[0], qzxB[0]], [sgB[1], qzxB[1]]]
        o0buf = sb("o0buf", [128, 4, 128], F32); o0B = [Buf() for _ in range(4)]
        o0flat = o0buf[:, :, :].rearrange("p a b -> p (a b)")
        etmp4 = etmp + [o0flat[:, 0:256], o0flat[:, 256:512]]; etmp4B = etmpB + [Buf(), Buf()]
        dbuf = [sb(f"dbuf{i}", [128, 128], F32) for i in range(4)]; dbufB = [Buf() for _ in range(4)]
        st = sb("st", [128, 32 * 24], F32)
        ring = [(st[:, i * 32:(i + 1) * 32], Buf()) for i in range(24)]
        cpos = sb("cpos", [128, NT, 8], F32); cposB = Buf()
        carry = sb("carry", [128, NT + 1, 8], F32); carryB = Buf()
        biasF = sb("biasF", [128, 4, NT, 8], F32); biasFB = Buf()

        mix = RW[:, 0:16384].rearrange("p (t n) -> p t n", t=NT); mixB = [Buf() for _ in range(NT)]
        wchunk = [RW[:, 16384 + i * 4096:16384 + (i + 1) * 4096].rearrange("p (k n) -> p k n", k=8) for i in range(2)]
        wchunkB = [Buf() for _ in range(2)]
        wout = RW[:, 16384:24576].rearrange("p (k n) -> p k n", k=8); woutB = Buf()
        Ebuf = RW[:, 24576:32768].bitcast(F32).rearrange("p (h e q) -> p h e q", h=16, e=2); EB = Buf()
        wd = RW[:, 0:22528].rearrange("p (c n) -> p c n", c=NCH); wdB = [Buf() for _ in range(2)]
        wgu = [[RW[:, 24576 + (i * 2 + j) * 2048:24576 + (i * 2 + j + 1) * 2048].rearrange("p (k n) -> p k n", k=8)
                for j in range(2)] for i in range(2)]
        wguB = [[Buf() for _ in range(2)] for _ in range(2)]
        qT0 = R2[:, 0:8192].rearrange("p (a n) -> p a n", a=4)
        kT0 = R2[:, 8192:16384].rearrange("p (a n) -> p a n", a=4)
        vF = R2[:, 16384:16384 + 8320].rearrange("p (t h e) -> p t h e", t=NT, h=8)
        vD = R2[:, 16384:16384 + 8256].rearrange("p (t h e) -> p t h e", t=NT, h=4)
        qT1 = R2[:, 0:16384].rearrange("p (a n) -> p a n", a=8)
        kT1 = R2[:, 16384:20480].rearrange("p (a n) -> p a n", a=2)
        vS = R2[:, 20480:20480 + 4160].rearrange("p (t h e) -> p t h e", t=NT, h=4)
        hidT = R2[:, 0:22528].rearrange("p (c n) -> p c n", c=NCH)
        qTB = [Buf() for _ in range(NT)]
        kTB = [Buf() for _ in range(NT)]
        vB = [Buf() for _ in range(NT)]
        vonesB = Buf()
        hidB = [[Buf() for _ in range(2)] for _ in range(NCH)]

        dramB = {}

        def dB(d, s_, t_):
            key = (id(d), s_, t_)
            if key not in dramB:
                dramB[key] = Buf()
            return dramB[key]

        marks = []

        def mark(lbl):
            marks.append((lbl, pe.n, act.n, dve.n))

        state = {"acc": 0, "qz": 0, "sT": 0, "qb": 0, "ring": 0, "tp": 0, "ps": 0, "xin": 0, "w": 0, "pt": 0, "et": 0}

        def smslot():
            r = ring[state["ring"] % len(ring)]
            state["ring"] += 1
            return r

        TPBANKS = [3, 7]

        def tpbank():
            i = TPBANKS[state["tp"] % 2]
            state["tp"] += 1
            return P[i][:, :].bitcast(BF16), PB[i]

        def tpbank_f32():
            i = TPBANKS[state["tp"] % 2]
            state["tp"] += 1
            return P[i], PB[i]

        def psbank():
            i = state["ps"] % 3
            state["ps"] += 1
            return P[i], PB[i]

        def nextxin():
            i = state["xin"] % 3
            state["xin"] += 1
            return xin[i], xinB[i]

        mk.dma(sp, small[:], small_d, writes=[smallB])
        mk.dma(sp, cpack[:], cpack_d, writes=[cpackB])
        mk.op(dve, lambda: V.tensor_copy(out=identb[:], in_=cpack[:, 0, :]), reads=[cpackB], writes=[constB])
        mk.op(dve, lambda: V.tensor_copy(out=tri0b[:], in_=cpack[:, 3, :]), reads=[cpackB], writes=[constB])
        mk.op(dve, lambda: V.memset(cst[:, 0:1], EPS), writes=[constB])
        mk.op(dve, lambda: V.memset(cst[:, 1:2], 1.0), writes=[constB])
        LAMBDA_INIT = 0.8 - 0.6 * 1.0
        lsl, lslB = smslot()
        prod = qn[0]
        mk.op(dve, lambda: V.tensor_tensor(out=prod[:, 0:64], in0=small[:, O_L:O_L + 64], in1=small[:, O_L + 64:O_L + 128], op=ALU.mult),
              reads=[smallB], writes=[qnB[0]])
        mk.op(dve, lambda: V.tensor_tensor(out=prod[:, 64:128], in0=small[:, O_L + 128:O_L + 192], in1=small[:, O_L + 192:O_L + 256], op=ALU.mult),
              reads=[smallB], writes=[qnB[0]])
        mk.op(dve, lambda: V.tensor_reduce(out=lsl[:, 0:2], in_=prod[:, 0:128].rearrange("p (a d) -> p a d", a=2), axis=AX.X, op=ALU.add),
              reads=[qnB[0]], writes=[lslB])
        mk.op(act, lambda: A.activation(out=lsl[:, 2:4], in_=lsl[:, 0:2], func=AF.Exp), reads=[lslB], writes=[lslB])
        mk.op(dve, lambda: V.tensor_tensor(out=lsl[:, 4:5], in0=lsl[:, 3:4], in1=lsl[:, 2:3], op=ALU.subtract), reads=[lslB], writes=[lslB])
        mk.op(dve, lambda: V.tensor_scalar(out=cst[:, 2:3], in0=lsl[:, 4:5], scalar1=-LAMBDA_INIT, scalar2=None, op0=ALU.add),
              reads=[lslB], writes=[constB])
        mk.op(dve, lambda: V.tensor_scalar(out=cst[:, 8:12], in0=small[:, O_B31:O_B31 + 4], scalar1=-1.0, scalar2=None, op0=ALU.mult),
              reads=[smallB], writes=[constB])
        mk.op(act, lambda: A.activation(out=small[:, O_SINK:O_SINK + 16], in_=small[:, O_SINK:O_SINK + 16], func=AF.Exp),
              reads=[smallB], writes=[smallB])
        mk.op(dve, lambda: V.tensor_copy(out=onesb[:], in_=cpack[:, 2, :]), reads=[cpackB], writes=[constB])
        mk.op(dve, lambda: V.tensor_copy(out=ehb[:], in_=small[:, O_SINK:O_SINK + 16]), reads=[smallB], writes=[constB])
        mk.op(dve, lambda: V.tensor_tensor(out=lsl[:, 8:24], in0=small[:, O_SINK:O_SINK + 16], in1=ehb[:], op=ALU.subtract), reads=[smallB, constB], writes=[lslB])
        mk.op(dve, lambda: V.tensor_scalar(out=ehb[0:64, :], in0=ehb[0:64, :], scalar1=1.0 / 64, scalar2=None, op0=ALU.mult), reads=[constB], writes=[constB])
        mk.op(dve, lambda: V.tensor_scalar(out=ehb[64:128, :], in0=lsl[64:128, 8:24], scalar1=1.0 / 64, scalar2=None, op0=ALU.mult), reads=[lslB, constB], writes=[constB])
        mk.op(dve, lambda: V.tensor_scalar(out=small[:, O_SUB:O_SUB + 128], in0=small[:, O_SUB:O_SUB + 128], scalar1=1.0 - LAMBDA_INIT,
                                           scalar2=None, op0=ALU.mult), reads=[smallB], writes=[smallB])
        for (oq, ok) in ((O_FQ, O_FK), (O_DQ, O_DK), (O_OQ, O_OK)):
            mk.op(dve, lambda oq=oq, ok=ok: V.tensor_tensor(out=small[:, ok:ok + 64], in0=small[:, ok:ok + 64], in1=small[:, oq:oq + 64], op=ALU.mult),
                  reads=[smallB], writes=[smallB])
        eps_ap = cst[:, 0:1]
        one_ap = cst[:, 1:2]
        neglam_ap = cst[:, 2:3]

        def rstd_chain(sl, slB, n, inv_n):
            mk.op(act, lambda: A.activation(out=sl[:, 8:8 + n], in_=sl[:, 0:n], func=AF.Ln, scale=inv_n, bias=eps_ap),
                  reads=[slB, constB], writes=[slB])
            mk.op(act, lambda: A.activation(out=sl[:, 16:16 + n], in_=sl[:, 8:8 + n], func=AF.Exp, scale=-0.5),
                  reads=[slB], writes=[slB])
            return sl[:, 16:16 + n]

        def norm_s1(xt, xtB, t):
            b = t % 2
            sl, slB = smslot()
            mk.op(act, lambda: A.activation(out=hb[b][:], in_=xt[:], func=AF.Square, accum_out=sl[:, 0:1]),
                  reads=[xtB], writes=[hbB[b], slB])
            r = rstd_chain(sl, slB, 1, 1.0 / D)
            mk.op(dve, lambda: V.scalar_tensor_tensor(out=hb[b][:], in0=xt[:], scalar=r, in1=gainT[:], op0=ALU.mult, op1=ALU.mult),
                  reads=[xtB, slB, gainB], writes=[hbB[b]])

        def norm_s2(t):
            b = t % 2
            tpv, tpB = tpbank()
            for k in range(8):
                mk.op(pe, lambda k=k: T.transpose(out=tpv[:, k * 128:(k + 1) * 128], in_=hb[b][:, k * 128:(k + 1) * 128], identity=identb[:]),
                      reads=[hbB[b], constB], writes=[tpB])
            mk.op(act, lambda: A.copy(out=hT[:, 0:4, t * 128:(t + 1) * 128], in_=tpv[:, 0:512].rearrange("p (k n) -> p k n", k=4)),
                  reads=[tpB], writes=[hTB[t]])
            mk.op(dve, lambda: V.tensor_copy(out=hT[:, 4:8, t * 128:(t + 1) * 128], in_=tpv[:, 512:1024].rearrange("p (k n) -> p k n", k=4)),
                  reads=[tpB], writes=[hTB[t]])

        def make_norm_pre(src_d, s, gi):
            mk.dma(sp, gainT[:], gains_d[:, gi, :], writes=[gainB])

            def s1(t):
                xt, xtB = nextxin()
                mk.dma(sp, xt[:], src_d[s, t * 128:(t + 1) * 128, :], reads=[dB(src_d, s, t)], writes=[xtB])
                norm_s1(xt, xtB, t)

            def pre(t):
                if t == 0:
                    s1(0)
                    s1(1)
                    norm_s2(0)
                if t + 1 < NT:
                    norm_s2(t + 1)
                if t + 2 < NT:
                    s1(t + 2)
            return pre

        def build_E_swa_head(h):
            mk.op(act, lambda: A.activation(out=Ebuf[:, h], in_=Ebuf[:, h], func=AF.Exp), reads=[EB], writes=[EB])
            mk.op(dve, lambda: V.tensor_tensor(out=Ebuf[:, h], in0=Ebuf[:, h], in1=cpack[:, 3:5, :], op=ALU.mult), reads=[EB, cpackB], writes=[EB])

        def proj_chunks(chunks, pre=None, lag=2, mid=None):
            def issue(c):
                i = state["w"] % 2
                state["w"] += 1
                mk.dma(pool, wchunk[i][:, :, 0:chunks[c][1]], chunks[c][0], writes=[wchunkB[i], wdB[0], wdB[1]])
                return i
            nxt = issue(0)
            pending = []
            for c in range(len(chunks)):
                cur = nxt
                if c + 1 < len(chunks):
                    nxt = issue(c + 1)
                ncols = chunks[c][1]
                for t in range(NT):
                    if c == 0 and pre is not None:
                        pre(t)
                    if c == 1 and mid is not None:
                        mid(t)
                    ps, psB = psbank()
                    for k in range(8):
                        mk.op(pe, lambda k=k: T.matmul(out=ps[:, 0:ncols], lhsT=hT[:, k, t * 128:(t + 1) * 128], rhs=wchunk[cur][:, k, 0:ncols],
                                                       start=(k == 0), stop=(k == 7)),
                              reads=[hTB[t], wchunkB[cur]], writes=[psB])
                    s2 = chunks[c][2](t, ps, psB)
                    if s2 is not None:
                        pending.append(s2)
                    while len(pending) > lag:
                        pending.pop(0)()
            while pending:
                pending.pop(0)()

        def evac_qknorm(gain_off, nheads, dst, dstB, col0=0, pair0=0, perm=False):
            def fn(t, ps, psB):
                b = state["qb"] % 3
                state["qb"] += 1
                n = nheads * 64
                src_ = ps[:, col0:col0 + n]
                sb_ = b % 2
                mk.op(act, lambda: A.activation(out=sq[sb_][:, 0:n], in_=src_, func=AF.Square), reads=[psB], writes=[sqB[sb_]])
                sl, slB = smslot()
                mk.op(dve, lambda: V.tensor_reduce(out=sl[:, 0:nheads], in_=sq[sb_][:, 0:n].rearrange("p (h d) -> p h d", d=64), axis=AX.X, op=ALU.add),
                      reads=[sqB[sb_]], writes=[slB])
                r = rstd_chain(sl, slB, nheads, 1.0 / 64)
                if gain_off is None:
                    if perm:
                        mk.op(dve, lambda: V.tensor_tensor(out=qb[b][:, 0:512].rearrange("p (a f d) -> p f a d", a=4, f=2, d=64),
                                                           in0=src_.rearrange("p (f a d) -> p f a d", f=2, a=4, d=64),
                                                           in1=r.rearrange("p (f a) -> p f a", f=2).unsqueeze(3).broadcast_to([128, 2, 4, 64]), op=ALU.mult),
                              reads=[psB, slB], writes=[qbB[b]])
                    else:
                        mk.op(dve, lambda: V.tensor_tensor(out=qb[b][:, 0:n].rearrange("p (h d) -> p h d", d=64), in0=src_.rearrange("p (h d) -> p h d", d=64),
                                                           in1=r.unsqueeze(2).broadcast_to([128, nheads, 64]), op=ALU.mult),
                              reads=[psB, slB], writes=[qbB[b]])
                else:
                    qi = b % 2
                    mk.op(dve, lambda: V.tensor_tensor(out=qn[qi][:, 0:n].rearrange("p (h d) -> p h d", d=64), in0=src_.rearrange("p (h d) -> p h d", d=64),
                                                       in1=r.unsqueeze(2).broadcast_to([128, nheads, 64]), op=ALU.mult),
                          reads=[psB, slB], writes=[qnB[qi]])
                    mk.op(dve, lambda: V.tensor_tensor(out=qb[b][:, 0:n].rearrange("p (h d) -> p h d", d=64), in0=qn[qi][:, 0:n].rearrange("p (h d) -> p h d", d=64),
                                                       in1=small[:, gain_off:gain_off + 64].unsqueeze(1).broadcast_to([128, nheads, 64]), op=ALU.mult),
                          reads=[qnB[qi], smallB], writes=[qbB[b]])
                npairs = nheads // 2

                def s2():
                    tpv, tpB = tpbank()
                    for pr in range(npairs):
                        mk.op(pe, lambda pr=pr: T.transpose(out=tpv[:, pr * 128:(pr + 1) * 128], in_=qb[b][:, pr * 128:(pr + 1) * 128], identity=identb[:]),
                              reads=[qbB[b], constB], writes=[tpB])
                    mk.op(act, lambda: A.copy(out=dst[:, pair0:pair0 + npairs, t * 128:(t + 1) * 128],
                                              in_=tpv[:, 0:npairs * 128].rearrange("p (a n) -> p a n", a=npairs)),
                          reads=[tpB], writes=[dstB[t]])
                return s2
            return fn

        def evac_v(vview, nheads, hd, col0=0):
            def fn(t, ps, psB):
                mk.op(act, lambda: A.copy(out=vview[:, t, :, 0:hd], in_=ps[:, col0:col0 + nheads * hd].rearrange("p (h d) -> p h d", d=hd)),
                      reads=[psB, vonesB], writes=[vB[t]])
            return fn

        def set_v_ones(vview, hd):
            mk.op(pool, lambda: G_.memset(vview[:, :, :, hd:hd + 1], 1.0), reads=[], writes=[vonesB] + vB)

        def evac_gate(t, ps, psB):
            sl, slB = smslot()
            mk.op(dve, lambda: V.tensor_tensor(out=sl[:, 0:8], in0=ps[:, 0:8], in1=small[:, O_BF:O_BF + 8], op=ALU.add),
                  reads=[psB, smallB], writes=[slB])
            mk.op(act, lambda: A.activation(out=sl[:, 8:16], in_=sl[:, 0:8], func=AF.Exp, scale=-1.0), reads=[slB], writes=[slB])
            mk.op(act, lambda: A.activation(out=sl[:, 16:24], in_=sl[:, 8:16], func=AF.Ln, scale=1.0, bias=one_ap), reads=[slB, constB], writes=[slB])

            def s2():
                cps, cpsB = tpbank_f32()
                mk.op(pe, lambda: T.matmul(out=cps[:, 0:8], lhsT=cpack[:, 1, :], rhs=sl[:, 16:24], start=True, stop=True), reads=[slB, cpackB], writes=[cpsB])
                mk.op(pe, lambda: T.matmul(out=cps[:, 8:16], lhsT=cpack[:, 2, :], rhs=sl[:, 16:24], start=True, stop=True), reads=[slB, cpackB], writes=[cpsB])
                mk.op(dve, lambda: V.tensor_tensor(out=cpos[:, t, :], in0=cps[:, 0:8], in1=carry[:, t, :], op=ALU.add), reads=[cpsB, carryB], writes=[cposB])
                mk.op(dve, lambda: V.tensor_tensor(out=carry[:, t + 1, :], in0=cps[:, 8:16], in1=carry[:, t, :], op=ALU.add), reads=[cpsB, carryB], writes=[carryB])
            return s2

        ACC = [4, 5, 6, 7]

        def sTbank():
            i = state["sT"] % 4
            state["sT"] += 1
            return P[i], PB[i]

        def zero_qz():
            for par in range(2):
                for i in range(2):
                    o = (1 - par) * 64
                    mk.op(pool, lambda par=par, i=i, o=o: G_.memset(qz[par][i][o:o + 64, :], 0.0), writes=[qzB[par][i]])

        def load_qz(par, pr, col0):
            i = state["qz"] % 2
            state["qz"] += 1
            o = par * 64
            t0_ = col0 // 128
            mk.op(pool, lambda: G_.tensor_copy(out=qz[par][i][o:o + 64, :], in_=qT0[o:o + 64, pr, col0:col0 + 512]),
                  reads=[qTB[t0_ + j] for j in range(4)], writes=[qzB[par][i]])
            return qz[par][i], qzB[par][i]

        def attn_stream(groups, qk_fn, step_fn, evac_fn, qz_ahead=6, qk_ahead=2):
            steps = [(g, kb) for g in groups for kb in range(g["nkb"])]
            ns = len(steps)
            qk_done = 0
            qz_done = 0
            late = []
            for si in range(ns):
                while qz_done < ns and qz_done <= si + qz_ahead:
                    g = steps[qz_done][0]
                    if "qzt" not in g:
                        g["qzt"], g["qztB"] = load_qz(g["par"], g["pr"], g["G"] * 512)
                    qz_done += 1
                while qk_done < ns and qk_done <= si + qk_ahead:
                    g, kb = steps[qk_done]
                    g.setdefault("pend", {})[kb] = qk_fn(g, kb)
                    qk_done += 1
                g, kb = steps[si]
                step_fn(g, kb, *g["pend"].pop(kb))
                while late and late[0][0] <= si:
                    late.pop(0)[1]()
                if kb == g["nkb"] - 1:
                    c = evac_fn(g)
                    if c is not None:
                        late.append((si + 8, c))
            while late:
                late.pop(0)[1]()

        def qk_common(g, kb):
            G = g["G"]
            r = kb - 4 * G
            c0 = max(r, 0) * 128
            sT, sTB = sTbank()
            mk.op(pe, lambda: T.matmul(out=sT[:, c0:512], lhsT=kT0[:, g["pr"], kb * 128:(kb + 1) * 128], rhs=g["qzt"][:, c0:512], start=True, stop=True),
                  reads=[kTB[kb], g["qztB"]], writes=[sTB])
            return sT, sTB, c0, r

        def fox_attention():
            for G in range(4):
                mk.op(dve, lambda G=G: V.tensor_tensor(out=biasF[:, G], in0=cpos[:], in1=carry[:, 4 * G + 2, :].unsqueeze(1).broadcast_to([128, NT, 8]),
                                                       op=ALU.subtract), reads=[cposB, carryB], writes=[biasFB])
            groups = []
            for h in range(8):
                for G in range(4):
                    groups.append({"h": h, "G": G, "par": h % 2, "pr": h // 2, "nkb": 4 * G + 4, "a": ACC[len(groups) % 4]})

            def step(g, kb, sT, sTB, c0, r):
                h, G, a = g["h"], g["G"], g["a"]
                pi = state["pt"] % NPT
                state["pt"] += 1
                mk.op(act, lambda: A.activation(out=PT[pi][:, c0:512], in_=sT[:, c0:512], func=AF.Exp, scale=SCALE, bias=biasF[:, G, kb, h:h + 1]),
                      reads=[sTB, biasFB], writes=[PTB[pi]])
                if r >= 0:
                    mk.op(dve, lambda: V.tensor_tensor(out=PT[pi][:, c0:c0 + 128], in0=PT[pi][:, c0:c0 + 128], in1=tri0b[:], op=ALU.mult),
                          reads=[PTB[pi], constB], writes=[PTB[pi]])
                for j in range(max(r, 0), 4):
                    mk.op(pe, lambda j=j: T.matmul(out=P[a][:, j * 65:(j + 1) * 65], lhsT=PT[pi][:, j * 128:(j + 1) * 128], rhs=vF[:, kb, h, :],
                                                   start=(kb == 0 and j == 0), stop=(kb == 4 * G + j), skip_group_check=True),
                          reads=[PTB[pi], vB[kb]], writes=[PB[a]])

            def evac(g):
                h, G, a = g["h"], g["G"], g["a"]
                sl, slB = smslot()
                accv = P[a][:, 0:260].rearrange("p (j e) -> p j e", e=65)
                mk.op(dve, lambda: V.reciprocal(out=sl[:, 0:4], in_=accv[:, :, 64]), reads=[PB[a]], writes=[slB])
                for j in range(4):
                    mk.op(dve, lambda j=j: V.tensor_scalar(out=mix[:, 4 * G + j, h * 64:(h + 1) * 64], in0=P[a][:, j * 65:j * 65 + 64], scalar1=sl[:, j:j + 1],
                                                           scalar2=None, op0=ALU.mult), reads=[PB[a], slB], writes=[mixB[4 * G + j]])

            attn_stream(groups, qk_common, step, evac)

        def load_E(bm_d, nh):
            mk.dma(pool, Ebuf[:, 0:nh], bm_d, writes=[EB] + [wguB[i][j] for i in range(2) for j in range(2)])

        def build_E(bm_d, nh, negb_col, use_tri1):
            for h in range(nh):
                if negb_col is not None:
                    mk.op(act, lambda h=h: A.activation(out=Ebuf[:, h], in_=Ebuf[:, h], func=AF.Exp, bias=cst[:, negb_col + h:negb_col + h + 1], scale=1.0),
                          reads=[EB, constB], writes=[EB])
                else:
                    mk.op(act, lambda h=h: A.activation(out=Ebuf[:, h], in_=Ebuf[:, h], func=AF.Exp), reads=[EB], writes=[EB])
            mk.op(dve, lambda: V.tensor_tensor(out=Ebuf[:, 0:nh, 0, :], in0=Ebuf[:, 0:nh, 0, :], in1=cpack[:, 3, :].unsqueeze(1).broadcast_to([128, nh, 128]),
                                               op=ALU.mult), reads=[EB, cpackB], writes=[EB])
            if use_tri1:
                mk.op(dve, lambda: V.tensor_tensor(out=Ebuf[:, 0:nh, 1, :], in0=Ebuf[:, 0:nh, 1, :], in1=cpack[:, 4, :].unsqueeze(1).broadcast_to([128, nh, 128]),
                                                   op=ALU.mult), reads=[EB, cpackB], writes=[EB])

        def diff_attention():
            groups = []
            for h in range(4):
                for G in range(4):
                    for m in range(2):
                        k2 = (len(groups) % 2) * 2
                        groups.append({"h": h, "G": G, "m": m, "par": m, "pr": h, "nkb": 4 * G + 4, "accs": [ACC[k2], ACC[k2 + 1]]})

            def step(g, kb, sT, sTB, c0, r):
                h, G, accs = g["h"], g["G"], g["accs"]
                Eh = Ebuf[:, h].rearrange("p e q -> p (e q)")
                pi = state["pt"] % NPT
                state["pt"] += 1
                far_lo = max(r + 2, 0)
                near_lo, near_hi = max(r, 0), min(r + 1, 3)
                bias_ap = small[:, O_B31 + h:O_B31 + h + 1]
                if far_lo <= 3:
                    f0 = far_lo * 128
                    mk.op(act, lambda: A.activation(out=PT[pi][:, f0:512], in_=sT[:, f0:512], func=AF.Exp, scale=SCALE, bias=bias_ap),
                          reads=[sTB, smallB], writes=[PTB[pi]])
                if near_hi >= near_lo:
                    n0, n1 = near_lo * 128, (near_hi + 1) * 128
                    e0 = (near_lo - r) * 128
                    ei = state["et"] % 2
                    state["et"] += 1
                    mk.op(act, lambda: A.activation(out=etmp[ei][:, 0:n1 - n0], in_=sT[:, n0:n1], func=AF.Exp, scale=SCALE, bias=bias_ap),
                          reads=[sTB, smallB], writes=[etmpB[ei]])
                    mk.op(dve, lambda: V.tensor_tensor(out=PT[pi][:, n0:n1], in0=etmp[ei][:, 0:n1 - n0], in1=Eh[:, e0:e0 + n1 - n0], op=ALU.mult),
                          reads=[etmpB[ei], EB], writes=[PTB[pi]])
                for j in range(max(r, 0), 4):
                    a = accs[j // 2]
                    jo = (j % 2) * 129
                    mk.op(pe, lambda a=a, jo=jo, j=j: T.matmul(out=P[a][:, jo:jo + 129], lhsT=PT[pi][:, j * 128:(j + 1) * 128], rhs=vD[:, kb, h, :],
                                                             start=(kb == 0 and j % 2 == 0), stop=(kb == 4 * G + j), skip_group_check=True),
                          reads=[PTB[pi], vB[kb]], writes=[PB[a]])

            def evac(g):
                h, G, m, accs = g["h"], g["G"], g["m"], g["accs"]
                sl, slB = smslot()
                for jj in range(2):
                    a = accs[jj]
                    mk.op(dve, lambda a=a, jj=jj: V.reciprocal(out=sl[:, 2 * jj:2 * jj + 2], in_=P[a][:, 0:258].rearrange("p (j e) -> p j e", e=129)[:, :, 128]),
                          reads=[PB[a]], writes=[slB])
                if m == 0:
                    for j in range(4):
                        a = accs[j // 2]
                        jo = (j % 2) * 129
                        mk.op(dve, lambda a=a, j=j, jo=jo: V.tensor_scalar(out=o0buf[:, j, :], in0=P[a][:, jo:jo + 128], scalar1=sl[:, j:j + 1], scalar2=None, op0=ALU.mult),
                              reads=[PB[a], slB], writes=[o0B[j]])
                else:
                    mk.op(dve, lambda: V.tensor_scalar(out=sl[:, 4:8], in0=sl[:, 0:4], scalar1=neglam_ap, scalar2=None, op0=ALU.mult), reads=[slB, constB], writes=[slB])
                    s2, s2B = smslot()
                    for j in range(4):
                        a = accs[j // 2]
                        jo = (j % 2) * 129
                        mk.op(dve, lambda a=a, j=j, jo=jo: V.scalar_tensor_tensor(out=dbuf[j][:], in0=P[a][:, jo:jo + 128], scalar=sl[:, 4 + j:5 + j], in1=o0buf[:, j, :],
                                                                                op0=ALU.mult, op1=ALU.add),
                              reads=[PB[a], slB, o0B[j]], writes=[dbufB[j]])
                        bq = j % 2
                        mk.op(dve, lambda j=j, bq=bq: V.scalar_tensor_tensor(out=sq[bq][:, 0:128], in0=dbuf[j][:], scalar=1.0, in1=dbuf[j][:],
                                                                           op0=ALU.mult, op1=ALU.mult, accum_out=s2[:, j:j + 1]),
                              reads=[dbufB[j]], writes=[sqB[bq], s2B])

                    def fin():
                        r2 = rstd_chain(s2, s2B, 4, 1.0 / 128)
                        for j in range(4):
                            mk.op(dve, lambda j=j: V.scalar_tensor_tensor(out=mix[:, 4 * G + j, 512 + h * 128:512 + (h + 1) * 128], in0=dbuf[j][:], scalar=r2[:, j:j + 1],
                                                                        in1=small[:, O_SUB:O_SUB + 128], op0=ALU.mult, op1=ALU.mult),
                                  reads=[dbufB[j], s2B, smallB], writes=[mixB[4 * G + j]])
                    return fin

            attn_stream(groups, qk_common, step, evac)

        def swa_attention():
            heads = []
            for hq in range(16):
                kvh = hq // 4
                cq, hl = hq // 8, hq % 8
                pr_q, pbq = 4 * cq + hl % 4, (hl // 4) * 64
                pr_k, pbk = kvh // 2, (kvh % 2) * 64
                assert pbq == pbk and pr_k == cq
                heads.append({"hq": hq, "kvh": kvh, "pr_q": pr_q, "pr_k": pr_k, "pbk": pbk, "pend": {}})
            steps = [(h, kb) for h in heads for kb in range(NT)]

            def qk(h, kb):
                pbk, pr_k = h["pbk"], h["pr_k"]
                if h["hq"] % 4 == 0 and kb == 0:
                    for hbi in range(2):
                        mk.op(pool, lambda hbi=hbi: G_.memset(hb[hbi][64 - pbk:128 - pbk, :], 0.0), writes=[hbB[hbi]])
                        mk.op(pool, lambda hbi=hbi: G_.tensor_copy(out=hb[hbi][pbk:pbk + 64, :], in_=kT1[pbk:pbk + 64, pr_k, hbi * 1024:(hbi + 1) * 1024]),
                              reads=[kTB[hbi * 8 + j] for j in range(8)], writes=[hbB[hbi]])
                ncols = 256 if kb + 1 < NT else 128
                sT, sTB = sTbank()
                mk.op(pe, lambda: T.matmul(out=sT[:, 0:ncols], lhsT=hb[kb // 8][:, (kb % 8) * 128:(kb % 8 + 1) * 128],
                                           rhs=qT1[:, h["pr_q"], kb * 128:kb * 128 + ncols], start=True, stop=True),
                      reads=[hbB[kb // 8], qTB[kb]] + ([qTB[kb + 1]] if kb + 1 < NT else []), writes=[sTB])
                h["pend"][kb] = (sT, sTB, ncols)

            def accof(t):
                return ACC[(t // 2) % 4], (t % 2) * 65

            def evac2(hq, t0_):
                a, _ = accof(t0_)
                sl, slB = smslot()
                av = P[a][:, 0:130].rearrange("p (j e) -> p j e", e=65)
                mk.op(dve, lambda: V.reciprocal(out=sl[:, 0:2], in_=av[:, :, 64]), reads=[PB[a]], writes=[slB])
                mk.op(dve, lambda: V.tensor_tensor(out=mix[:, t0_:t0_ + 2, hq * 64:(hq + 1) * 64], in0=av[:, :, 0:64],
                                                   in1=sl[:, 0:2].unsqueeze(2).broadcast_to([128, 2, 64]), op=ALU.mult),
                      reads=[PB[a], slB], writes=[mixB[t0_], mixB[t0_ + 1]])

            qk_done = 0
            for si, (h, kb) in enumerate(steps):
                while qk_done < len(steps) and qk_done <= si + 3:
                    qk(*steps[qk_done])
                    qk_done += 1
                hq, kvh = h["hq"], h["kvh"]
                Eh = Ebuf[:, hq].rearrange("p e q -> p (e q)")
                sT, sTB, ncols = h["pend"].pop(kb)
                pi = state["pt"] % NPT
                state["pt"] += 1
                ei = state["et"] % 4
                state["et"] += 1
                mk.op(act, lambda: A.activation(out=etmp4[ei][:, 0:ncols], in_=sT[:, 0:ncols], func=AF.Exp, scale=SCALE), reads=[sTB], writes=[etmp4B[ei]])
                if kb % 2 == 0:
                    mk.op(dve, lambda: V.tensor_tensor(out=PT[pi][:, 0:ncols], in0=etmp4[ei][:, 0:ncols], in1=Eh[:, 0:ncols], op=ALU.mult),
                          reads=[etmp4B[ei], EB], writes=[PTB[pi]])
                else:
                    mk.op(pool, lambda: G_.tensor_tensor(out=PT[pi][:, 0:ncols], in0=etmp4[ei][:, 0:ncols], in1=Eh[:, 0:ncols], op=ALU.mult),
                          reads=[etmp4B[ei], EB], writes=[PTB[pi]])
                a, ao = accof(kb)
                mk.op(pe, lambda: T.matmul(out=P[a][:, ao:ao + 65], lhsT=PT[pi][:, 0:128], rhs=vS[:, kb, kvh, :], start=(kb == 0), stop=False, skip_group_check=True),
                      reads=[PTB[pi], vB[kb]], writes=[PB[a]])
                mk.op(pe, lambda: T.matmul(out=P[a][:, ao + 64:ao + 65], lhsT=onesb[:], rhs=ehb[:, hq:hq + 1], start=False, stop=True, skip_group_check=True),
                      reads=[constB], writes=[PB[a]])
                if kb + 1 < NT:
                    a2, ao2 = accof(kb + 1)
                    mk.op(pe, lambda: T.matmul(out=P[a2][:, ao2:ao2 + 65], lhsT=PT[pi][:, 128:256], rhs=vS[:, kb, kvh, :], start=((kb + 1) % 2 == 0), stop=False,
                                               skip_group_check=True),
                          reads=[PTB[pi], vB[kb]], writes=[PB[a2]])
                if kb >= 2 and kb % 2 == 0:
                    evac2(hq, kb - 2)
                if kb == NT - 1:
                    evac2(hq, NT - 2)

        def phase_outproj(w_d, src_d, dst_d, s, gi):
            mk.dma(pool, wout[:, :, 0:512], w_d[:, :, 0:512], writes=[woutB, wchunkB[0], wchunkB[1]])
            mk.dma(pool, wout[:, :, 512:1024], w_d[:, :, 512:1024], writes=[woutB, wchunkB[0], wchunkB[1]])
            mk.dma(sp, gainT[:], gains_d[:, gi, :], writes=[gainB])
            xts = {}

            def stT(t):
                b = t % 2
                tpv, tpB = tpbank()
                for k in range(8):
                    mk.op(pe, lambda k=k: T.transpose(out=tpv[:, k * 128:(k + 1) * 128], in_=mix[:, t, k * 128:(k + 1) * 128], identity=identb[:]),
                          reads=[mixB[t], constB], writes=[tpB])
                mk.op(act, lambda: A.copy(out=mT[b][:, 0:4, :], in_=tpv[:, 0:512].rearrange("p (k n) -> p k n", k=4)), reads=[tpB], writes=[mTB[b]])
                mk.op(dve, lambda: V.tensor_copy(out=mT[b][:, 4:8, :], in_=tpv[:, 512:1024].rearrange("p (k n) -> p k n", k=4)), reads=[tpB], writes=[mTB[b]])
                xt, xtB = nextxin()
                mk.dma(sp, xt[:], src_d[s, t * 128:(t + 1) * 128, :], reads=[dB(src_d, s, t)], writes=[xtB])
                xts[t] = (xt, xtB)

            def stM(t):
                b = t % 2
                xt, xtB = xts[t]
                for half in range(2):
                    ps, psB = psbank()
                    for k in range(8):
                        mk.op(pe, lambda k=k: T.matmul(out=ps[:, :], lhsT=mT[b][:, k, :], rhs=wout[:, k, half * 512:(half + 1) * 512], start=(k == 0), stop=(k == 7)),
                              reads=[mTB[b], woutB], writes=[psB])
                    mk.op(dve, lambda: V.tensor_tensor(out=xt[:, half * 512:(half + 1) * 512], in0=ps[:, :], in1=xt[:, half * 512:(half + 1) * 512], op=ALU.add),
                          reads=[psB, xtB], writes=[xtB])
                mk.dma(sp, dst_d[s, t * 128:(t + 1) * 128, :], xt[:], reads=[xtB], writes=[dB(dst_d, s, t)])
                norm_s1(xt, xtB, t)

            stT(0)
            for t in range(NT):
                if t + 1 < NT:
                    stT(t + 1)
                stM(t)
                if t >= 1:
                    norm_s2(t - 1)
            norm_s2(NT - 1)

        def phase_ffn(layer, src_d, dst_d, s):
            wg_d = w_gate_d[layer].rearrange("(k p) n -> p k n", p=128)
            wu_d = w_up_d[layer].rearrange("(k p) n -> p k n", p=128)
            wd_d = w_down_d[layer].rearrange("(c p) n -> p c n", p=128)
            blocks = [(i * 256, 256) for i in range(11)]

            def issue_gu(bi):
                i = state["w"] % 2
                state["w"] += 1
                c0, n = blocks[bi]
                mk.dma(pool, wgu[i][0][:, :, 0:n], wg_d[:, :, c0:c0 + n], writes=[wguB[i][0], EB])
                mk.dma(pool, wgu[i][1][:, :, 0:n], wu_d[:, :, c0:c0 + n], writes=[wguB[i][1], EB])
                return i

            for hh in range(2):
                nxt = issue_gu(0)
                wd_parts = [(half, cc) for half in range(2) for cc in range(2)] if hh == 0 else []
                for bi in range(len(blocks)):
                    cur = nxt
                    if bi + 1 < len(blocks):
                        nxt = issue_gu(bi + 1)
                    if bi >= 1 and wd_parts:
                        half, cc = wd_parts.pop(0)
                        mk.dma(pool, wd[:, cc * 11:(cc + 1) * 11, half * 512:(half + 1) * 512], wd_d[:, cc * 11:(cc + 1) * 11, half * 512:(half + 1) * 512],
                               writes=[wdB[half], woutB, wchunkB[0], wchunkB[1]] + mixB)
                    for j in range(blocks[bi][1] // 128):
                        c = blocks[bi][0] // 128 + j
                        for tg in range(2):
                            tok0 = hh * 1024 + tg * 512
                            gi_ = (c * 2 + tg) % 2
                            gps, gB = P[gi_], PB[gi_]
                            ups, uB = P[2 + gi_], PB[2 + gi_]
                            rd = [hTB[tok0 // 128 + q] for q in range(4)]
                            for k in range(8):
                                mk.op(pe, lambda k=k: T.matmul(out=gps[:, :], lhsT=wgu[cur][0][:, k, j * 128:(j + 1) * 128], rhs=hT[:, k, tok0:tok0 + 512],
                                                               start=(k == 0), stop=(k == 7)), reads=rd + [wguB[cur][0]], writes=[gB])
                            for k in range(8):
                                mk.op(pe, lambda k=k: T.matmul(out=ups[:, :], lhsT=wgu[cur][1][:, k, j * 128:(j + 1) * 128], rhs=hT[:, k, tok0:tok0 + 512],
                                                               start=(k == 0), stop=(k == 7)), reads=rd + [wguB[cur][1]], writes=[uB])
                            mk.op(act, lambda: A.activation(out=sg[gi_][:], in_=gps[:, :], func=AF.Silu), reads=[gB], writes=[sgB[gi_]])
                            mk.op(dve, lambda: V.tensor_tensor(out=hidT[:, c, tg * 512:(tg + 1) * 512], in0=ups[:, :], in1=sg[gi_][:], op=ALU.mult),
                                  reads=[uB, sgB[gi_]], writes=[hidB[c][tg]])
                for tt in range(8):
                    t = hh * 8 + tt
                    xt, xtB = nextxin()
                    mk.dma(sp, xt[:], src_d[s, t * 128:(t + 1) * 128, :], reads=[dB(src_d, s, t)], writes=[xtB])
                    for half in range(2):
                        a = ACC[(tt * 2 + half) % 4]
                        for c in range(NCH):
                            mk.op(pe, lambda c=c, a=a: T.matmul(out=P[a][:, :], lhsT=hidT[:, c, tt * 128:(tt + 1) * 128], rhs=wd[:, c, half * 512:(half + 1) * 512],
                                                                start=(c == 0), stop=(c == NCH - 1)),
                                  reads=[hidB[c][tt // 4], wdB[half]], writes=[PB[a]])
                        mk.op(dve, lambda a=a: V.tensor_tensor(out=xt[:, half * 512:(half + 1) * 512], in0=P[a][:, :], in1=xt[:, half * 512:(half + 1) * 512], op=ALU.add),
                              reads=[PB[a], xtB], writes=[xtB])
                    mk.dma(sp, dst_d[s, t * 128:(t + 1) * 128, :], xt[:], reads=[xtB], writes=[dB(dst_d, s, t)])

        allhid = [hidB[c][tg] for c in range(NCH) for tg in range(2)]
        r2bufs = qTB + kTB + vB + [vonesB]

        def fence(eng, bufs):
            if eng is dve:
                mk.op(dve, lambda: V.memset(cst[:, 12:13], 0.0), writes=list(bufs))
            elif eng is pool:
                mk.op(pool, lambda: G_.memset(cst[:, 13:14], 0.0), writes=list(bufs))
            else:
                mk.op(act, lambda: A.copy(out=cst[:, 14:15], in_=cst[:, 1:2]), reads=[constB], writes=list(bufs))

        def fence_before_ffn():
            fence(dve, r2bufs)

        def fence_after_ffn():
            fence(act, allhid)
            fence(pool, allhid)
            fence(dve, allhid + [wdB[0], wdB[1]])

        def zero_carry():
            mk.op(dve, lambda: V.memset(carry[:, 0, :], 0.0), writes=[carryB])

        def dst_of(stage):
            return out_d if stage == STAGES else r_d[stage - 1]

        for s in range(NSEQ):
            src = x_d
            mark("projF"); set_v_ones(vF, 64)
            load_E(bm_diff_d, 4)
            zero_carry()
            proj_chunks(pre=make_norm_pre(src, s, 0), chunks=[
                (w_in_d[:, :, 0:512], 512, evac_qknorm(None, 8, qT0, qTB)),
                (w_in_d[:, :, 512:1024], 512, evac_qknorm(O_FK, 8, kT0, kTB)),
                (w_in_d[:, :, 1024:1536], 512, evac_v(vF, 8, 64)),
                (w_in_d[:, :, 1536:1544], 8, evac_gate),
            ])
            mark("fox"); zero_qz(); build_E(bm_diff_d, 4, 8, False); fox_attention()
            mark("projD"); set_v_ones(vD, 128)
            proj_chunks([
                (w_in_d[:, :, 1544:2056], 512, evac_qknorm(None, 8, qT0, qTB)),
                (w_in_d[:, :, 2056:2568], 512, evac_qknorm(O_DK, 8, kT0, kTB)),
                (w_in_d[:, :, 2568:3080], 512, evac_v(vD, 4, 128)),
            ])
            mark("diff"); diff_attention()
            mark("out0"); phase_outproj(w_out0_d, src, dst_of(1), s, 2)
            if STAGES >= 2:
                fence_before_ffn()
                mark("ffn0"); phase_ffn(0, dst_of(1), dst_of(2), s)
                fence_after_ffn()
            if STAGES >= 3:
                src = dst_of(2)
                set_v_ones(vS, 64)
                load_E(bm_swa_d, 16)
                kfn = evac_qknorm(O_OK, 4, kT1, kTB)
                vfn = evac_v(vS, 4, 64, col0=256)

                def kv_evac(t, ps, psB):
                    s2 = kfn(t, ps, psB)
                    vfn(t, ps, psB)
                    return s2
                mark("projS")
                proj_chunks(pre=make_norm_pre(src, s, 1), mid=build_E_swa_head, chunks=[
                    (w_qkv_d[:, :, 0:512], 512, evac_qknorm(None, 8, qT1, qTB, pair0=0, perm=True)),
                    (w_qkv_d[:, :, 512:1024], 512, evac_qknorm(None, 8, qT1, qTB, pair0=4, perm=True)),
                    (w_qkv_d[:, :, 1024:1536], 512, kv_evac),
                ])
                mark("swa"); swa_attention()
                mark("out1"); phase_outproj(w_out1_d, src, dst_of(3), s, 3)
            if STAGES >= 4:
                fence_before_ffn()
                mark("ffn1"); phase_ffn(1, dst_of(3), dst_of(4), s)
                fence_after_ffn()
            elif STAGES == 3 or STAGES == 1:
                mk.barrier()
        mark("end")
        mk.finish()
        build_nc.marks = marks
        build_nc.stats = {e.name: e.n for e in (pe, act, dve, pool)}
        build_nc.stats["waits"] = mk.nwaits
    return nc


def _t5_bucket(delta):
    n = np.maximum(delta, 0)
    nf = np.maximum(n, 1).astype(np.float32)
    large = 16 + (np.log(nf / np.float32(16)) / np.float32(np.log(8.0)) * np.float32(16)).astype(np.int32)
    large = np.minimum(large, 31)
    return np.where(n < 16, n, large)


def host_prep(inputs):
    f = lambda a: np.ascontiguousarray(np.asarray(a, dtype=np.float32))
    table = f(inputs["rel_bias_table"])
    gains = np.stack([f(inputs["ev_attn_norm"])[0], f(inputs["od_attn_norm"])[0], f(inputs["ffn_norm"])[0], f(inputs["ffn_norm"])[1]])
    gains = np.ascontiguousarray(np.broadcast_to(gains[None], (128, 4, D)))
    sm = np.zeros(NSM, np.float32)
    sm[O_FQ:O_FQ + 64] = f(inputs["ev_fox_q_norm"])[0]
    sm[O_FK:O_FK + 64] = f(inputs["ev_fox_k_norm"])[0]
    sm[O_DQ:O_DQ + 64] = f(inputs["ev_diff_q_norm"])[0]
    sm[O_DK:O_DK + 64] = f(inputs["ev_diff_k_norm"])[0]
    sm[O_OQ:O_OQ + 64] = f(inputs["od_q_norm"])[0]
    sm[O_OK:O_OK + 64] = f(inputs["od_k_norm"])[0]
    sm[O_SUB:O_SUB + 128] = f(inputs["ev_diff_subln"])[0]
    sm[O_L:O_L + 64] = f(inputs["ev_lambda_q1"])[0]
    sm[O_L + 64:O_L + 128] = f(inputs["ev_lambda_k1"])[0]
    sm[O_L + 128:O_L + 192] = f(inputs["ev_lambda_q2"])[0]
    sm[O_L + 192:O_L + 256] = f(inputs["ev_lambda_k2"])[0]
    sm[O_BF:O_BF + 8] = f(inputs["ev_b_forget"])[0]
    sm[O_SINK:O_SINK + 16] = f(inputs["od_sinks"])[0]
    sm[O_B31:O_B31 + 4] = table[31, 0:4]
    small = np.ascontiguousarray(np.broadcast_to(sm[None], (128, NSM)))
    k = np.arange(128)[:, None]
    q = np.arange(128)[None, :]
    cpack = np.zeros((128, 5, 128), np.float32)
    cpack[:, 0] = np.eye(128, dtype=np.float32)
    cpack[:, 1] = (k <= q)
    cpack[:, 2] = 1.0
    cpack[:, 3] = (k <= q)
    cpack[:, 4] = (k > q)
    idx = np.stack([_t5_bucket(q - k), _t5_bucket(128 + q - k)], axis=0)
    bm = table[idx]
    bm = np.ascontiguousarray(np.transpose(bm, (1, 3, 0, 2)))
    common = {
        "w_in": f(inputs["ev_w_in"])[0], "w_out0": f(inputs["ev_w_out"])[0],
        "w_qkv": f(inputs["od_w_qkv"])[0], "w_out1": f(inputs["od_w_out"])[0],
        "w_gate": f(inputs["w_gate"]), "w_up": f(inputs["w_up"]), "w_down": f(inputs["w_down"]),
        "gains": gains, "small": small, "cpack": cpack,
        "bm_diff": np.ascontiguousarray(bm[:, 0:4]), "bm_swa": bm,
    }
    return common


def kernel(**inputs):
    x = np.ascontiguousarray(np.asarray(inputs["x"], dtype=np.float32))
    common = host_prep(inputs)
    per = x.shape[0] // NCORES
    nc = build_nc(NSEQ=per, STAGES=4)
    in_maps = []
    for c in range(NCORES):
        m = dict(common)
        m["x"] = np.ascontiguousarray(x[c * per:(c + 1) * per])
        in_maps.append(m)
    res = run_bass_kernel_spmd(nc, in_maps, core_ids=list(range(NCORES)))
    return np.concatenate([np.asarray(r["out"], dtype=np.float32) for r in res.results], axis=0)
```
